# Optimizing a Trainium2 kernel written in Bass

```python
import jax, jax.numpy as jnp
from jax import lax
import numpy as np

D_MODEL = 1024
BATCH = 16
SEQ = 256
DEPTH = 2
DEC_BATCH = 2
DEC_SEQ = 1024
PAST_LEN = 256

GRID_W = 64
N_MIXERS = 2
BRANCH_WIDTH = D_MODEL
N_FOURIER_GROUPS = 4
HEAD_DIM = 64
N_HEADS = BRANCH_WIDTH // HEAD_DIM
N_KV_HEADS = 4
GQA_GROUP = N_HEADS // N_KV_HEADS
KV_WIDTH = N_KV_HEADS * HEAD_DIM
WINDOW = 128
BLOCK = 128
ROPE_THETA = 10000.0
EPS = 1e-6
NEG_INF = -1e30

kernel_name = "hybrid_fnet_swa_dit_step"


def rms_norm(x, w):
    xf = x.astype(jnp.float32)
    y = xf * lax.rsqrt(jnp.mean(xf * xf, axis=-1, keepdims=True) + EPS)
    return (y * w.astype(jnp.float32)).astype(x.dtype)


def modulation(cond, w_mod, b_mod):
    m = jax.nn.silu(cond) @ w_mod + b_mod
    shift, scale, gate = jnp.split(m[:, None, :], 3, axis=-1)
    return shift, scale, gate


def modulated_norm(x, cond, norm_w, w_mod, b_mod):
    shift, scale, gate = modulation(cond, w_mod, b_mod)
    return rms_norm(x, norm_w) * (1 + scale) + shift, gate


def fourier_mix(u):
    b, s, e = u.shape
    ug = u.astype(jnp.float32).reshape(b, s, N_FOURIER_GROUPS, e // N_FOURIER_GROUPS)
    f = jnp.fft.fft2(ug, axes=(1, 3), norm="ortho")
    return jnp.real(f).reshape(b, s, e).astype(u.dtype)


def fourier_layer(x, cond, norm_w, w_mod, b_mod, w_in, w_out):
    h, gate = modulated_norm(x, cond, norm_w, w_mod, b_mod)
    u, z = jnp.split(h @ w_in, 2, axis=-1)
    y = fourier_mix(u) * jax.nn.silu(z)
    return x + gate * (y @ w_out)


def attn_project(h, w_in, q_norm_w, k_norm_w):
    b, s, _ = h.shape
    q, k, v, z = jnp.split(h @ w_in, [BRANCH_WIDTH, BRANCH_WIDTH + KV_WIDTH,
                                      BRANCH_WIDTH + 2 * KV_WIDTH], axis=-1)
    q = rms_norm(q.reshape(b, s, N_KV_HEADS, GQA_GROUP, HEAD_DIM), q_norm_w)
    k = rms_norm(k.reshape(b, s, N_KV_HEADS, HEAD_DIM), k_norm_w)
    v = v.reshape(b, s, N_KV_HEADS, HEAD_DIM)
    return q, k, v, z


def axial_rope_tables(s):
    rows = s // GRID_W
    row = jnp.repeat(jnp.arange(rows, dtype=jnp.float32), GRID_W)
    col = jnp.tile(jnp.arange(GRID_W, dtype=jnp.float32), rows)
    n_freq = HEAD_DIM // 4
    inv = ROPE_THETA ** (-jnp.arange(n_freq, dtype=jnp.float32) / n_freq)
    ang = jnp.concatenate([row[:, None] * inv, col[:, None] * inv], axis=-1)
    return jnp.cos(ang), jnp.sin(ang)


def apply_rope(x, cos, sin):
    shp = (cos.shape[0],) + (1,) * (x.ndim - 3) + (cos.shape[1],)
    c, s_ = cos.reshape(shp), sin.reshape(shp)
    xf = x.astype(jnp.float32)
    x1, x2 = jnp.split(xf, 2, axis=-1)
    return jnp.concatenate([x1 * c - x2 * s_, x1 * s_ + x2 * c], axis=-1).astype(x.dtype)


def sink_softmax(scores, sink):
    s_col = jnp.broadcast_to(sink.astype(jnp.float32).reshape(N_KV_HEADS, GQA_GROUP, 1, 1),
                             scores.shape[:-1] + (1,))
    p = jax.nn.softmax(jnp.concatenate([scores, s_col], axis=-1), axis=-1)
    return p[..., :-1]


def context_attention(q, k, v, sink):
    b, s = q.shape[:2]
    nq = s // BLOCK
    scale = HEAD_DIM ** -0.5
    qb = jnp.moveaxis(q.reshape(b, nq, BLOCK, N_KV_HEADS, GQA_GROUP, HEAD_DIM), 1, 0)
    kf, vf = k.astype(jnp.float32), v.astype(jnp.float32)

    def one_block(qi):
        sc = jnp.einsum("bqkgd,bjkd->bkgqj", qi.astype(jnp.float32), kf) * scale
        p = sink_softmax(sc, sink)
        return jnp.einsum("bkgqj,bjkd->bqkgd", p, vf)

    o = lax.map(one_block, qb)
    return jnp.moveaxis(o, 0, 1).reshape(b, s, BRANCH_WIDTH).astype(q.dtype)


def latent_attention(q, k, v, k_ctx, v_ctx, sink):
    b, s = q.shape[:2]
    nb = s // BLOCK
    scale = HEAD_DIM ** -0.5
    pad = ((0, 0), (BLOCK, BLOCK), (0, 0), (0, 0))
    kp = jnp.pad(k.astype(jnp.float32), pad).reshape(b, nb + 2, BLOCK, N_KV_HEADS, HEAD_DIM)
    vp = jnp.pad(v.astype(jnp.float32), pad).reshape(b, nb + 2, BLOCK, N_KV_HEADS, HEAD_DIM)
    kb = jnp.concatenate([kp[:, :-2], kp[:, 1:-1], kp[:, 2:]], axis=2)
    vb = jnp.concatenate([vp[:, :-2], vp[:, 1:-1], vp[:, 2:]], axis=2)
    qb = q.astype(jnp.float32).reshape(b, nb, BLOCK, N_KV_HEADS, GQA_GROUP, HEAD_DIM)
    kc, vc = k_ctx.astype(jnp.float32), v_ctx.astype(jnp.float32)

    s_loc = jnp.einsum("bnqkgd,bnjkd->bnkgqj", qb, kb) * scale
    n_i = jnp.arange(nb)[:, None, None]
    q_i = jnp.arange(BLOCK)[None, :, None]
    k_j = jnp.arange(3 * BLOCK)[None, None, :]
    kpos = n_i * BLOCK + k_j - BLOCK
    qpos = n_i * BLOCK + q_i
    valid = (jnp.abs(kpos - qpos) <= WINDOW) & (kpos >= 0) & (kpos < s)
    s_loc = jnp.where(valid[None, :, None, None], s_loc, NEG_INF)
    s_ctx = jnp.einsum("bnqkgd,bjkd->bnkgqj", qb, kc) * scale

    p = sink_softmax(jnp.concatenate([s_loc, s_ctx], axis=-1), sink)
    p_loc, p_ctx = p[..., :3 * BLOCK], p[..., 3 * BLOCK:]
    o = (jnp.einsum("bnkgqj,bnjkd->bnqkgd", p_loc, vb)
         + jnp.einsum("bnkgqj,bjkd->bnqkgd", p_ctx, vc))
    return o.reshape(b, s, BRANCH_WIDTH).astype(q.dtype)


def attn_layer_context(x, cond, norm_w, w_mod, b_mod, w_in, q_norm_w, k_norm_w, sink, w_out):
    h, gate = modulated_norm(x, cond, norm_w, w_mod, b_mod)
    q, k, v, z = attn_project(h, w_in, q_norm_w, k_norm_w)
    o = context_attention(q, k, v, sink)
    return x + gate * ((o * jax.nn.silu(z)) @ w_out), k, v


def attn_layer_latent(x, cond, k_ctx, v_ctx, norm_w, w_mod, b_mod, w_in, q_norm_w, k_norm_w,
                      sink, w_out):
    h, gate = modulated_norm(x, cond, norm_w, w_mod, b_mod)
    q, k, v, z = attn_project(h, w_in, q_norm_w, k_norm_w)
    cos, sin = axial_rope_tables(x.shape[1])
    q, k = apply_rope(q, cos, sin), apply_rope(k, cos, sin)
    o = latent_attention(q, k, v, k_ctx, v_ctx, sink)
    return x + gate * ((o * jax.nn.silu(z)) @ w_out)


def setup_inputs(seed: int = 0) -> dict:
    key = jax.random.key(seed)
    ks = jax.random.split(key, 24)
    f32 = jnp.float32
    D, E = D_MODEL, BRANCH_WIDTH
    nrm = lambda k, shape, s: jax.random.normal(k, shape, f32) * s
    return {
        "x_prompt": nrm(ks[0], (BATCH, SEQ, D), 1.0),
        "x_sample": nrm(ks[1], (DEC_BATCH, DEC_SEQ, D), 1.0),
        "cache_k_l1": nrm(ks[2], (DEC_BATCH, PAST_LEN, N_KV_HEADS, HEAD_DIM), 1.0),
        "cache_v_l1": nrm(ks[3], (DEC_BATCH, PAST_LEN, N_KV_HEADS, HEAD_DIM), 1.0),
        "c": nrm(ks[4], (DEC_BATCH, D), 1.0),
        "c_ctx": nrm(ks[5], (D,), 1.0),
        "norm_w_l0": 1.0 + nrm(ks[6], (D,), 0.02),
        "w_mod_l0": nrm(ks[7], (D, 3 * D), 0.5 * D ** -0.5),
        "b_mod_l0": nrm(ks[8], (3 * D,), 0.02),
        "w_in_l0": nrm(ks[9], (D, 2 * E), D ** -0.5),
        "w_out_l0": nrm(ks[10], (E, D), E ** -0.5),
        "norm_w_l1": 1.0 + nrm(ks[11], (D,), 0.02),
        "w_mod_l1": nrm(ks[12], (D, 3 * D), 0.5 * D ** -0.5),
        "b_mod_l1": nrm(ks[13], (3 * D,), 0.02),
        "w_in_l1": nrm(ks[14], (D, 2 * E + 2 * KV_WIDTH), D ** -0.5),
        "q_norm_w_l1": 1.0 + nrm(ks[15], (HEAD_DIM,), 0.02),
        "k_norm_w_l1": 1.0 + nrm(ks[16], (HEAD_DIM,), 0.02),
        "sink_l1": nrm(ks[17], (N_HEADS,), 0.5),
        "w_out_l1": nrm(ks[18], (E, D), E ** -0.5),
    }


def reference(x_prompt, x_sample, cache_k_l1, cache_v_l1, c, c_ctx,
              norm_w_l0, w_mod_l0, b_mod_l0, w_in_l0, w_out_l0,
              norm_w_l1, w_mod_l1, b_mod_l1, w_in_l1, q_norm_w_l1, k_norm_w_l1, sink_l1,
              w_out_l1):
    fourier_params = (norm_w_l0, w_mod_l0, b_mod_l0, w_in_l0, w_out_l0)
    attn_params = (norm_w_l1, w_mod_l1, b_mod_l1, w_in_l1, q_norm_w_l1, k_norm_w_l1, sink_l1,
                   w_out_l1)
    layer_params = (fourier_params, attn_params)
    cond_ctx = c_ctx[None, :]
    xp, xs = x_prompt, x_sample
    new_k_l1, new_v_l1 = None, None
    for i in range(DEPTH):
        p = layer_params[i]
        if i % N_MIXERS == 0:
            xp = fourier_layer(xp, cond_ctx, *p)
            xs = fourier_layer(xs, c, *p)
        else:
            xp, new_k_l1, new_v_l1 = attn_layer_context(xp, cond_ctx, *p)
            xs = attn_layer_latent(xs, c, cache_k_l1, cache_v_l1, *p)
    return (xp, xs, new_k_l1, new_v_l1)
```

```python
import numpy as np
import ml_dtypes
from contextlib import ExitStack
import concourse.bass as bass
import concourse.mybir as mybir
from concourse.bass_utils import run_bass_kernel_spmd

F32 = mybir.dt.float32
BF16 = mybir.dt.bfloat16
AF = mybir.ActivationFunctionType
ALU = mybir.AluOpType
AX = mybir.AxisListType

SB_BASE = 18432
SB_END = 229376
NCORES = 8
SPLIT_SAMPLE_FRONT = False


class Buf:
    def __init__(self, space, lo, hi, apv, esz):
        self.space, self.lo, self.hi, self.apv, self.esz = space, lo, hi, apv, esz

    def ap(self):
        return self.apv

    def __getitem__(self, k):
        return self.apv[k]

    def col(self, k0, k1):
        return Buf(self.space, self.lo + k0 * self.esz, self.lo + k1 * self.esz,
                   self.apv[:, k0:k1], self.esz)


class Op:
    __slots__ = ("eng", "fn", "reads", "writes", "dma", "deps", "idx", "milestone", "semkey", "cnt")

    def __init__(self, eng, fn, reads, writes, dma, semkey):
        self.eng, self.fn, self.reads, self.writes, self.dma = eng, fn, reads, writes, dma
        self.deps = set()
        self.idx = None
        self.milestone = False
        self.semkey = semkey
        self.cnt = None


class Sched:
    ENGS = ("pe", "act", "dve", "pool", "sp")

    def __init__(self, nc):
        self.nc = nc
        self.ops = []
        self.sb_ptr = SB_BASE
        self.hw = SB_BASE
        self.state = {}
        self.nbuf = 0
        self.nkey = 0

    def sb(self, name, shape, dtype, at=None):
        esz = 2 if dtype == BF16 else 4
        n = int(np.prod(shape[1:])) * esz
        n_al = (n + 31) // 32 * 32
        if at is None:
            off = self.sb_ptr
            self.sb_ptr += n_al
        else:
            off = at
        assert off + n <= SB_END, (name, off, n, SB_END)
        self.hw = max(self.hw, off + n)
        self.nbuf += 1
        h = self.nc.alloc_sbuf_tensor_at(f"{name}_{self.nbuf}", list(shape), dtype, offset=off)
        return Buf("sb", off, off + n, h.ap(), esz)

    def mark(self):
        return self.sb_ptr

    def reset(self, m):
        self.sb_ptr = m

    def op(self, eng, fn, reads=(), writes=(), dma=False, semkey=None):
        if dma and semkey is None:
            self.nkey += 1
            semkey = f"k{self.nkey}"
        o = Op(eng, fn, list(reads), list(writes), dma, semkey)
        st_items = list(self.state.items())
        for b in o.reads:
            for key, st in st_items:
                if key[0] == b.space and key[1] < b.hi and b.lo < key[2]:
                    if st[0] is not None:
                        o.deps.add(st[0])
        for b in o.writes:
            for key, st in st_items:
                if key[0] == b.space and key[1] < b.hi and b.lo < key[2]:
                    if st[0] is not None:
                        o.deps.add(st[0])
                    for r in st[1]:
                        o.deps.add(r)
        o.deps.discard(o)
        for b in o.reads:
            key = (b.space, b.lo, b.hi)
            st = self.state.setdefault(key, [None, []])
            if not o.dma:
                st[1] = [r for r in st[1] if r.dma or r.eng != o.eng]
            st[1].append(o)
        for b in o.writes:
            key = (b.space, b.lo, b.hi)
            for k2 in list(self.state.keys()):
                if k2 != key and k2[0] == b.space and b.lo <= k2[1] and k2[2] <= b.hi:
                    del self.state[k2]
            self.state[key] = [o, []]
        self.ops.append(o)
        return o

    def emit(self):
        nc = self.nc
        for o in self.ops:
            for d in o.deps:
                d.milestone = True
        cnt = {e: 0 for e in self.ENGS}
        dcnt = {}
        for o in self.ops:
            if o.dma:
                dcnt[o.semkey] = dcnt.get(o.semkey, 0) + 1
                o.cnt = dcnt[o.semkey]
            elif o.milestone:
                cnt[o.eng] += 1
                o.idx = cnt[o.eng]
        sem_names = [("c_" + e) for e in self.ENGS if e != "sp"] + ["d_" + str(k) for k in dcnt]
        self.n_sems = len(sem_names)
        sems = {}
        with ExitStack() as es:
            for n in sem_names:
                sems[n] = es.enter_context(nc.semaphore(n))
            block = es.enter_context(nc.Block())
            per = {e: [o for o in self.ops if o.eng == e] for e in self.ENGS}
            final_waits = [("d_" + str(k), 16 * v) for k, v in dcnt.items()]

            def run(eng_name, eng):
                known = {}
                for o in per[eng_name]:
                    need = {}
                    for d in o.deps:
                        if d.dma:
                            s, v = "d_" + str(d.semkey), 16 * d.cnt
                        else:
                            if d.eng == "pe" and eng_name == "pe" and not o.dma:
                                continue
                            s, v = "c_" + d.eng, d.idx
                        if need.get(s, 0) < v:
                            need[s] = v
                    for s, v in need.items():
                        if known.get(s, 0) < v:
                            eng.wait_ge(sems[s], v)
                            known[s] = v
                    ins = o.fn(eng)
                    if o.dma:
                        ins.then_inc(sems["d_" + str(o.semkey)], 16)
                    elif o.milestone:
                        ins.then_inc(sems["c_" + eng_name], 1)
                if eng_name == "sp":
                    for s, v in final_waits:
                        if known.get(s, 0) < v:
                            eng.wait_ge(sems[s], v)

            @block.sync
            def _(e):
                run("sp", e)

            @block.tensor
            def _(e):
                run("pe", e)

            @block.scalar
            def _(e):
                run("act", e)

            @block.vector
            def _(e):
                run("dve", e)

            @block.gpsimd
            def _(e):
                run("pool", e)


def build_program():
    nc = bass.Bass("TRN2", target_bir_lowering=False)
    S = Sched(nc)

    def din(name, shape, dt=F32):
        return nc.dram_tensor(name, list(shape), dt, kind="ExternalInput").ap()

    def dout(name, shape):
        return nc.dram_tensor(name, list(shape), F32, kind="ExternalOutput").ap()

    d_xp = din("xp", [512, 1024])
    d_xs = din("xs", [1024, 1024])
    d_condT = din("condT", [128, 8, 2])
    d_ck = din("ck", [256, 256])
    d_cv = din("cv", [256, 256])
    d_wmod = [din("wmod0", [1024, 3072]), din("wmod1", [1024, 3072])]
    d_bmod = [din("bmod0", [3072]), din("bmod1", [3072])]
    d_win0 = din("win0", [1024, 2048])
    d_wout0 = din("wout0", [1024, 1024])
    d_win1 = din("win1", [1024, 2560])
    d_wout1 = din("wout1", [1024, 1024])
    d_nwT = din("nwT", [128, 2, 8])
    d_qkw = din("qkw", [128])
    d_sink = din("sink", [16])
    d_identb = din("identb", [128, 128], BF16)
    d_identf = din("identf", [2, 2])
    d_sel = din("sel", [2, 2, 128])
    d_dftP = din("dftP", [256, 2, 256], BF16)
    d_dftC = din("dftC", [256, 2, 256], BF16)
    d_dftS = din("dftS", [1024, 2, 512], BF16)
    d_rope = din("rope", [512, 2, 64])
    d_masks = din("masks", [128, 4, 512], BF16)
    o_yp = dout("yp", [512, 1024])
    o_ys = dout("ys", [256, 1024])
    o_nk = dout("nk", [512, 256])
    o_nv = dout("nv", [512, 256])

    PS = []
    for i in range(8):
        h = nc.alloc_psum_tensor(f"ps{i}", [128, 512], F32)
        PS.append(Buf("ps", i * 2048, (i + 1) * 2048, h.ap(), 4))
    ring = {"g": [2, 3, 4, 5, 6, 7], "gi": 0, "t": [0, 1], "ti": 0}

    def nb():
        b = PS[ring["g"][ring["gi"] % len(ring["g"])]]
        ring["gi"] += 1
        return b

    def ntb():
        b = PS[ring["t"][ring["ti"] % 2]]
        ring["ti"] += 1
        return b

    XP = [S.sb(f"xp{t}", [128, 1024], F32) for t in range(4)]
    XL = [S.sb(f"xl{t}", [128, 1024], F32) for t in range(4)]
    wAu = [S.sb(f"wAu{i}", [128, 8, 512], BF16) for i in range(2)]
    wAz = [S.sb(f"wAz{i}", [128, 8, 512], BF16) for i in range(2)]
    wCq = [S.sb(f"wCq{i}", [128, 8, 512], BF16) for i in range(2)]
    wCkv = S.sb("wCkv", [128, 8, 512], BF16)
    wm = [S.sb(f"wm{i}", [128, 8, 256], BF16) for i in range(2)]
    bm = [S.sb(f"bm{i}", [2, 256], F32) for i in range(2)]
    mrow = [S.sb(f"mrow{i}", [2, 256], F32) for i in range(2)]
    gate = S.sb("gate", [128, 2, 1024], F32)
    MT = [S.sb(f"mt{l}", [128, 2, 8, 2], F32) for l in range(2)]
    identb = S.sb("identb", [128, 128], BF16)
    identf = S.sb("identf", [2, 2], F32)
    sel = S.sb("sel", [2, 2, 128], F32)
    nwT = S.sb("nwT", [128, 2, 8], F32)
    condT = S.sb("condT", [128, 8, 2], F32)
    scT = S.sb("scT", [128, 8, 2], BF16)
    ssb = S.sb("ssb", [128, 160], F32)
    rsb = S.sb("rsb", [128, 160], F32)
    mhalf = S.sb("mhalf", [128, 16], F32)
    ones3 = S.sb("ones3", [128, 192], BF16)
    qkw = S.sb("qkw", [128, 2, 64], F32)
    esink = S.sb("esink", [128, 16], F32)
    esinkP = S.sb("esinkP", [128, 4, 2], F32)
    off_wB = S.mark()
    wB = S.sb("wB", [128, 8, 1024], BF16)
    off_dftS = S.mark()
    dftS = S.sb("dftS", [128, 8, 2, 512], BF16)
    xn = [S.sb(f"xn{i}", [128, 1024], BF16) for i in range(2)]
    tmp = [S.sb(f"tmp{i}", [128, 512], F32) for i in range(2)]
    off_L = S.mark()
    dftP = S.sb("dftP", [128, 2, 2, 256], BF16)
    dftC = S.sb("dftC", [128, 2, 2, 256], BF16)
    hT = S.sb("hT", [128, 8, 1024], BF16)
    off_U = S.mark()
    U = S.sb("U", [128, 8, 512], BF16)
    AT = S.sb("AT", [128, 4, 2, 512], BF16)
    zs = [S.sb(f"zs{i}", [128, 512], BF16) for i in range(2)]
    yT = S.sb("yT", [128, 8, 512], BF16)
    xin = [S.sb(f"xin{i}", [128, 1024], F32) for i in range(2)]
    end_L0 = S.mark()
    gate1 = S.sb("gate1", [128, 2, 1024], F32, at=xin[0].lo)
    assert xin[1].hi == xin[0].lo + 8192
    gates = [gate, gate1]
    wm0 = wm + [S.sb("wmx0", [128, 8, 256], BF16, at=off_U), S.sb("wmx1", [128, 8, 256], BF16, at=off_U + 4096)]

    cnt = {"ss": 0, "ev": 0, "xn": 0, "tmp": 0, "wm": 0, "zs": 0, "bm": 0, "xin": 0}
    vcnt = {"n": 0}

    def vbuf(apv):
        vcnt["n"] += 1
        return Buf("v", vcnt["n"] * 10, vcnt["n"] * 10 + 1, apv, 2)

    hT_A, hT_B, hT_C = vbuf(hT.ap()), vbuf(hT.ap()), vbuf(hT.ap())

    def hT_of(col0, ncols=128):
        out = []
        if col0 < 256:
            out.append(hT_A)
        if col0 < 512 and col0 + ncols > 256:
            out.append(hT_B)
        if col0 + ncols > 512:
            out.append(hT_C)
        return out

    def dma(q, out_ap, in_ap, reads=(), writes=(), key=None):
        S.op(q, lambda e: e.dma_start(out=out_ap, in_=in_ap), reads=reads, writes=writes, dma=True, semkey=key)

    def mm(bank, out_ap, lhsT, rhs, start, stop, reads, sgc=False):
        if sgc:
            S.op("pe", lambda e: e.matmul(out_ap, lhsT=lhsT, rhs=rhs, start=start, stop=stop, skip_group_check=True),
                 reads=reads, writes=[bank])
        else:
            S.op("pe", lambda e: e.matmul(out_ap, lhsT=lhsT, rhs=rhs, start=start, stop=stop),
                 reads=reads, writes=[bank])

    def tr(bank, out_ap, in_ap, ident_ap, reads):
        S.op("pe", lambda e: e.transpose(out=out_ap, in_=in_ap, identity=ident_ap), reads=reads, writes=[bank])

    def evac(dst_ap, src_ap, reads, writes, eng=None):
        if eng is None:
            eng = "act" if cnt["ev"] % 2 == 0 else "dve"
            cnt["ev"] += 1
        if eng == "act":
            S.op("act", lambda e: e.activation(out=dst_ap, in_=src_ap, func=AF.Identity), reads=reads, writes=writes)
        else:
            S.op(eng, lambda e: e.tensor_copy(out=dst_ap, in_=src_ap), reads=reads, writes=writes)

    def next_ss(n=1):
        k = cnt["ss"]
        cnt["ss"] += n
        assert cnt["ss"] <= 160
        return ssb.col(k, k + n), rsb.col(k, k + n)

    def rstd_ops(ssc, rsc, n, inv):
        S.op("dve", lambda e: e.tensor_scalar(out=rsc.ap(), in0=ssc.ap(), scalar1=inv, scalar2=1e-6,
                                              op0=ALU.mult, op1=ALU.add), reads=[ssc], writes=[rsc])
        S.op("pool", lambda e: e.tensor_tensor(out=rsc.ap(), in0=rsc.ap(), in1=mhalf[:, 0:n], op=ALU.pow),
             reads=[rsc, mhalf], writes=[rsc])

    dma("sp", identb.ap(), d_identb, writes=[identb])
    dma("sp", identf.ap(), d_identf, writes=[identf])
    dma("sp", sel.ap(), d_sel, writes=[sel])
    dma("sp", nwT.ap(), d_nwT, writes=[nwT])
    dma("sp", condT.ap(), d_condT, writes=[condT])
    dma("sp", qkw.ap(), d_qkw.partition_broadcast(128).rearrange("p (a d) -> p a d", a=2), writes=[qkw])
    dma("sp", esink.ap(), d_sink.partition_broadcast(128), writes=[esink])
    S.op("pool", lambda e: e.memset(ssb.ap(), 0.0), writes=[ssb])
    S.op("pool", lambda e: e.memset(mhalf.ap(), -0.5), writes=[mhalf])
    S.op("pool", lambda e: e.memset(mhalf[:, 12:13], 1e-6), writes=[mhalf])
    S.op("pool", lambda e: e.memset(ones3.ap(), 1.0), writes=[ones3])
    S.op("pool", lambda e: e.memset(ones3[:, 64:128], 0.0), writes=[ones3])
    S.op("act", lambda e: e.activation(out=esink.ap(), in_=esink.ap(), func=AF.Exp), reads=[esink], writes=[esink])
    esv = esink.ap().rearrange("p (g a two) -> p g a two", g=4, a=2)
    S.op("dve", lambda e: e.tensor_copy(out=esinkP[0:64, :, :], in_=esv[0:64, :, :, 0]), reads=[esink], writes=[esinkP])
    S.op("dve", lambda e: e.tensor_copy(out=esinkP[64:128, :, :], in_=esv[64:128, :, :, 1]), reads=[esink],
         writes=[esinkP])
    S.op("act", lambda e: e.activation(out=scT.ap(), in_=condT.ap(), func=AF.Silu), reads=[condT], writes=[scT])
    for t in range(4):
        dma("sp", XP[t].ap(), d_xp[t * 128:(t + 1) * 128, :], writes=[XP[t]])

    def deferred_loads():
        for t in range(4):
            dma("sp", XL[t].ap(), d_xs[t * 128:(t + 1) * 128, :], reads=[wAz[1]], writes=[XL[t]])
        dma("sp", dftS.ap(), d_dftS.rearrange("(t p) f n -> p t f n", p=128), reads=[wAz[1]], writes=[dftS])

    mod_state = {}

    def mod_load(l, j, bufs):
        i = cnt["wm"] % len(bufs)
        cnt["wm"] += 1
        ib = cnt["bm"] % 2
        cnt["bm"] += 1
        dma("pool", bufs[i].ap(), d_wmod[l][:, j * 256:(j + 1) * 256].rearrange("(kc p) n -> p kc n", p=128),
            writes=[bufs[i]], key=f"wm{i}")
        mod_state[(l, j)] = (bufs[i], ib)

    def mod_a(l, j):
        wbuf, ib = mod_state[(l, j)]
        dma("sp", bm[ib].ap(), d_bmod[l][j * 256:(j + 1) * 256].partition_broadcast(2), writes=[bm[ib]],
            key=f"bm{ib}")
        bank = nb()
        for kc in range(8):
            mm(bank, bank[0:2, 0:256], scT[:, kc, :], wbuf[:, kc, :], kc == 0, kc == 7, [scT, wbuf])
        S.op("dve", lambda e: e.tensor_tensor(out=mrow[ib].ap(), in0=bank[0:2, 0:256], in1=bm[ib].ap(), op=ALU.add),
             reads=[bank, bm[ib]], writes=[mrow[ib]])

    def mod_b(l, j):
        wbuf, ib = mod_state[(l, j)]
        which = j // 4
        if which < 2:
            b2 = nb()
            for h in range(2):
                tr(b2, b2[:, h * 2:(h + 1) * 2], mrow[ib][:, h * 128:(h + 1) * 128], identf.ap(), [mrow[ib], identf])
            kc0 = (j % 4) * 2
            src = b2[:, 0:4].rearrange("p (h r) -> p h r", h=2)
            if which == 0:
                S.op("dve", lambda e: e.tensor_copy(out=MT[l][:, 0, kc0:kc0 + 2, :], in_=src),
                     reads=[b2], writes=[MT[l]])
            else:
                nwb = nwT[:, l, kc0:kc0 + 2].unsqueeze(2).to_broadcast([128, 2, 2])
                S.op("dve", lambda e: e.scalar_tensor_tensor(out=MT[l][:, 1, kc0:kc0 + 2, :], in0=src, scalar=1.0,
                                                             in1=nwb, op0=ALU.add, op1=ALU.mult),
                     reads=[b2, nwT], writes=[MT[l]])
        else:
            for r in range(2):
                b3 = nb()
                mm(b3, b3[:, 0:256], sel[:, r, :], mrow[ib].ap(), True, True, [sel, mrow[ib]])
                c0 = (j - 8) * 256
                evac(gates[l][:, r, c0:c0 + 256], b3[:, 0:256], [b3], [gates[l]], eng="act")

    def prep_A(xbuf, on_dve=False):
        ssc, rsc = next_ss()
        xi = xn[cnt["xn"] % 2]
        cnt["xn"] += 1
        if on_dve:
            S.op("dve", lambda e: e.scalar_tensor_tensor(out=xi.ap(), in0=xbuf.ap(), scalar=1.0, in1=xbuf.ap(),
                                                         op0=ALU.mult, op1=ALU.mult, accum_out=ssc.ap()),
                 reads=[xbuf], writes=[xi, ssc])
            rstd_ops(ssc, rsc, 1, 1.0 / 1024)
            S.op("dve", lambda e: e.tensor_scalar(out=xi.ap(), in0=xbuf.ap(), scalar1=rsc.ap(), scalar2=None,
                                                  op0=ALU.mult), reads=[xbuf, rsc], writes=[xi])
            return xi
        S.op("act", lambda e: e.activation(out=xi.ap(), in_=xbuf.ap(), func=AF.Square, accum_out=ssc.ap()),
             reads=[xbuf], writes=[xi, ssc])
        rstd_ops(ssc, rsc, 1, 1.0 / 1024)
        S.op("act", lambda e: e.activation(out=xi.ap(), in_=xbuf.ap(), func=AF.Identity, scale=rsc.ap()),
             reads=[xbuf, rsc], writes=[xi])
        return xi

    def prep_B(l, xi, r, hbuf, col0, hdeps=None, shift_eng="dve"):
        hdeps = hdeps if hdeps is not None else [hbuf]
        tb = ntb()
        tbv = tb.ap().bitcast(BF16)
        for kc in range(8):
            tr(tb, tbv[:, kc * 128:(kc + 1) * 128], xi[:, kc * 128:(kc + 1) * 128], identb.ap(), [xi, identb])
        hv = hbuf[:, :, col0:col0 + 128]
        scb = MT[l][:, 1, :, r:r + 1].to_broadcast([128, 8, 128])
        shb = MT[l][:, 0, :, r:r + 1].to_broadcast([128, 8, 128])
        S.op("dve", lambda e: e.tensor_tensor(out=hv, in0=tbv.rearrange("p (k q) -> p k q", k=8), in1=scb, op=ALU.mult),
             reads=[tb, MT[l]], writes=hdeps)
        S.op(shift_eng, lambda e: e.tensor_tensor(out=hv, in0=hv, in1=shb, op=ALU.add), reads=hdeps + [MT[l]],
             writes=hdeps)

    def prep_seq(l, tiles, r, hbuf, rest_src=None, pre=None, cols=None, depf=None, tidx=None):
        nt = len(tiles)
        xis = list(pre) if pre else []
        for step in range(nt + 1):
            if step < nt and step >= len(xis):
                xb = tiles[step]
                if xb is None:
                    xb = xin[cnt["xin"] % 2]
                    dma("sp", xb.ap(), rest_src(tidx[step]), writes=[xb], key=f"xin{cnt['xin'] % 2}")
                    cnt["xin"] += 1
                xis.append(prep_A(xb))
            if step >= 1:
                c0 = cols[step - 1] if cols else (step - 1) * 128
                prep_B(l, xis[step - 1], r, hbuf, c0, depf(c0) if depf else None)

    def prep_stages(l, tiles, r, hbuf, cols, depf, rest_src=None, tidx=None, dve_prep=False):
        st = []
        xis = {}
        nt = len(tiles)

        def mk(step):
            def f():
                if step >= 2:
                    prep_B(l, xis[step - 2], r, hbuf, cols[step - 2], depf(cols[step - 2]),
                           shift_eng=("pool" if (l == 1 and dve_prep) else "dve"))
                if step < nt:
                    xb = tiles[step]
                    if xb is None:
                        xb = xin[cnt["xin"] % 2]
                        dma("sp", xb.ap(), rest_src(tidx[step]), writes=[xb], key=f"xin{cnt['xin'] % 2}")
                        cnt["xin"] += 1
                    xis[step] = prep_A(xb, on_dve=(l == 1 and dve_prep and step % 2 == 0))
            return f

        for step in range(nt + 2):
            st.append(mk(step))
        return st

    def residual_update(bank, xbuf, r, n, l=0):
        tp = tmp[cnt["tmp"] % 2]
        cnt["tmp"] += 1
        gl = gates[l]
        S.op("dve", lambda e: e.tensor_tensor(out=tp.ap(), in0=bank[:, 0:512], in1=gl[:, r, n * 512:(n + 1) * 512],
                                              op=ALU.mult), reads=[bank, gl], writes=[tp])
        S.op("pool", lambda e: e.tensor_tensor(out=xbuf[:, n * 512:(n + 1) * 512], in0=xbuf[:, n * 512:(n + 1) * 512],
                                               in1=tp.ap(), op=ALU.add), reads=[xbuf, tp], writes=[xbuf])

    def l0_main(nt, n_local, dft_ap, dft_buf, hooks=None, cb=0, stage_q=None):
        ncols = n_local * 128
        stage_q = list(stage_q or [])
        n_stage = len(stage_q)
        n_points = 2 * (nt + (4 if ncols == 256 else 8) + 4)
        pt = {"i": 0, "done": 0}

        def point():
            pt["i"] += 1
            target = (pt["i"] * n_stage) // n_points
            while pt["done"] < target and stage_q:
                stage_q.pop(0)()
                pt["done"] += 1
        for gp in range(2):
            for t in range(nt):
                bank = nb()
                for kc in range(8):
                    mm(bank, bank[:, 0:512], hT[:, kc, cb + t * 128:cb + (t + 1) * 128],
                       wAu[gp][:, kc, :], kc == 0, kc == 7, hT_of(cb + t * 128) + [wAu[gp]])
                evac(U[:, t, :], bank[:, 0:512], [bank], [U])
                point()
            for cc in range(4):
                if ncols == 256:
                    bank = nb()
                    for f in range(2):
                        for t in range(nt):
                            mm(bank, bank[:, f * 256:(f + 1) * 256], U[:, t, cc * 128:(cc + 1) * 128], dft_ap(t, f),
                               f == 0 and t == 0, t == nt - 1, [U, dft_buf], sgc=True)
                    evac(AT[:, cc, :, 0:256], bank[:, 0:512].rearrange("p (f q) -> p f q", f=2), [bank], [AT])
                    point()
                    continue
                for f in range(2):
                    bank = nb()
                    for t in range(nt):
                        mm(bank, bank[:, 0:ncols], U[:, t, cc * 128:(cc + 1) * 128], dft_ap(t, f),
                           t == 0, t == nt - 1, [U, dft_buf])
                    evac(AT[:, cc, f, 0:ncols], bank[:, 0:ncols], [bank], [AT])
                    point()
            if hooks and ("afterA", gp) in hooks:
                hooks[("afterA", gp)]()
            for e2 in range(4):
                e_ = gp * 4 + e2
                g, cp = e_ // 2, e_ % 2
                zb = nb()
                for kc in range(8):
                    mm(zb, zb[:, 0:ncols], wAz[gp][:, kc, e2 * 128:(e2 + 1) * 128], hT[:, kc, cb:cb + ncols],
                       kc == 0, kc == 7, [wAz[gp]] + hT_of(cb, ncols))
                zi = zs[cnt["zs"] % 2]
                cnt["zs"] += 1
                S.op("act", lambda e, zb=zb, zi=zi: e.activation(out=zi[:, 0:ncols], in_=zb[:, 0:ncols], func=AF.Silu),
                     reads=[zb], writes=[zi])
                yb = nb()
                k = 0
                for cch in range(2):
                    for f in range(2):
                        mm(yb, yb[:, 0:ncols], dftC[:, cch, f, cp * 128:(cp + 1) * 128],
                           AT[:, (g - gp * 2) * 2 + cch, f, 0:ncols], k == 0, k == 3, [dftC, AT])
                        k += 1
                S.op("dve", lambda e, yb=yb, zi=zi, e_=e_: e.tensor_tensor(out=yT[:, e_, 0:ncols], in0=yb[:, 0:ncols],
                                                                           in1=zi[:, 0:ncols], op=ALU.mult),
                     reads=[yb, zi], writes=[yT])
                if hooks and ("afterE", e_) in hooks:
                    hooks[("afterE", e_)]()
                point()
        while stage_q:
            stage_q.pop(0)()
        if hooks and "afterZ" in hooks:
            hooks["afterZ"]()

    def l0_out(tiles, r, n_local, side=()):
        side = list(side)
        n_side, n_grp, gi_, done_ = len(side), 2 * n_local, 0, 0
        for t in range(n_local):
            for n in range(2):
                gi_ += 1
                while side and done_ < -(-(gi_ * n_side) // n_grp):
                    side.pop(0)()
                    done_ += 1
                ob = nb()
                for kc in range(8):
                    mm(ob, ob[:, 0:512], yT[:, kc, t * 128:(t + 1) * 128], wB[:, kc, n * 512:(n + 1) * 512],
                       kc == 0, kc == 7, [yT, wB])
                residual_update(ob, tiles[t], r, n)
        for f_ in side:
            f_()

    for j in range(4):
        mod_load(0, j, wm0)
    pre0 = [prep_A(XP[0]), prep_A(XP[1])]
    for j in range(8):
        mod_a(0, j)
        mod_b(0, j)
        if j + 4 < 10:
            mod_load(0, j + 4, wm0)
        if j == 3:
            dma("pool", wAu[0].ap(), d_win0[:, 0:512].rearrange("(kc p) n -> p kc n", p=128), writes=[wAu[0]])
        if j == 2:
            dma("sp", dftP.ap(), d_dftP.rearrange("(t p) f n -> p t f n", p=128), writes=[dftP])
            dma("sp", dftC.ap(), d_dftC.rearrange("(t p) f n -> p t f n", p=128), writes=[dftC])
    dma("pool", wAz[0].ap(), d_win0[:, 1024:1536].rearrange("(kc p) n -> p kc n", p=128), reads=[MT[0]], writes=[wAz[0]])
    dma("pool", wAu[1].ap(), d_win0[:, 512:1024].rearrange("(kc p) n -> p kc n", p=128), reads=[MT[0]], writes=[wAu[1]])

    def late_weights(which):
        if which == 0:
            dma("pool", wAz[1].ap(), d_win0[:, 1536:2048].rearrange("(kc p) n -> p kc n", p=128), reads=[wAz[0]],
                writes=[wAz[1]])
        else:
            dma("pool", wB.ap(), d_wout0.rearrange("(kc p) n -> p kc n", p=128), reads=[wAu[1]], writes=[wB])

    rest_src = lambda t: d_xs[t * 128:(t + 1) * 128, :]
    prep_seq(0, [XP[0], XP[1]], 0, hT, pre=pre0, cols=[0, 128], depf=hT_of)
    hk0 = {}
    st1_ = prep_stages(0, [XP[2], XP[3]], 0, hT, [256, 384], hT_of)

    def mk_a(l, j, nxt):
        def hk():
            mod_a(l, j)
            if nxt is not None:
                mod_load(l, nxt, wm)
        return hk

    def mk_b(l, j):
        return lambda: mod_b(l, j)

    q0_ = []
    for j in range(8, 12):
        q0_ += [mk_a(0, j, j + 2 if j + 2 < 12 else None), mk_b(0, j)]
    def mk_both(e_):
        def hk():
            q0_[e_]()
            if e_ == 1:
                late_weights(0)
            if e_ == 2:
                late_weights(1)
            if e_ in (1, 3, 5, 7):
                st1_[(e_ - 1) // 2]()
        return hk

    for e_ in range(8):
        hk0[("afterE", e_)] = mk_both(e_)
    l0_main(2, 2, lambda t, f: dftP[:, t, f, :], dftP, cb=0, hooks=hk0)
    cnt["wm"] = 0
    q1_ = []
    for j in range(12):
        q1_ += [mk_a(1, j, j + 2 if j + 2 < 12 else None), mk_b(1, j)]

    def mk_multi(fs):
        return lambda: [f() for f in fs]

    l0_out([XP[0], XP[1]], 0, 2,
           side=prep_stages(0, [None] * 4, 1, hT, [512, 640, 768, 896], hT_of, rest_src=rest_src, tidx=[4, 5, 6, 7]))
    deferred_loads()
    l0_main(2, 2, lambda t, f: dftP[:, t, f, :], dftP, cb=256)
    mod_load(1, 0, wm)
    mod_load(1, 1, wm)
    l0_out([XP[2], XP[3]], 0, 2, side=prep_stages(0, XL, 1, hT, [0, 128, 256, 384], hT_of))

    hooks = {}
    sq_ = list(q1_)
    sq_.insert(4, lambda: dma("pool", wCkv.ap(), d_win1[:, 1024:1536].rearrange("(kc p) n -> p kc n", p=128),
                              writes=[wCkv]))
    sq_.insert(11, lambda: dma("pool", wCq[0].ap(), d_win1[:, 0:512].rearrange("(kc p) n -> p kc n", p=128),
                               writes=[wCq[0]]))
    sq_.insert(18, lambda: dma("pool", wCq[1].ap(), d_win1[:, 512:1024].rearrange("(kc p) n -> p kc n", p=128),
                               writes=[wCq[1]]))
    wCz_holder = {}

    def load_wCz():
        wCz_holder["b"] = S.sb("wCz", [128, 8, 1024], BF16, at=off_dftS)
        dma("pool", wCz_holder["b"].ap(), d_win1[:, 1536:2560].rearrange("(kc p) n -> p kc n", p=128),
            writes=[wCz_holder["b"]])

    hooks[("afterA", 1)] = load_wCz
    ckb = S.sb("ckb", [128, 2, 256], BF16, at=wAz[0].lo + 12288)
    cvb = S.sb("cvb", [128, 2, 256], BF16, at=wAz[0].lo + 13312)

    def after_z():
        dma("pool", ckb.ap(), d_ck.rearrange("(t p) n -> p t n", p=128), writes=[ckb])
        dma("pool", cvb.ap(), d_cv.rearrange("(t p) n -> p t n", p=128), writes=[cvb])
        for n_ in range(2):
            dma("pool", wAu[n_].ap(), d_wout1[:, n_ * 512:(n_ + 1) * 512].rearrange("(kc p) n -> p kc n", p=128),
                writes=[wAu[n_]])

    hooks["afterZ"] = after_z
    l0_main(8, 4, lambda t, f: dftS[:, t, f, :], dftS, hooks=hooks, stage_q=sq_)
    wCz = wCz_holder["b"]

    S.reset(off_wB)
    QT = [S.sb(f"QT{i}", [64, 16, 128], BF16) for i in range(2)]
    KT = S.sb("KT", [64, 4, 512], BF16)
    VVc = S.sb("VVc", [128, 2, 4, 192], BF16)
    assert S.mark() <= off_dftS
    S.reset(wAz[0].lo)
    masks = S.sb("masks", [128, 4, 512], BF16)
    qf2 = S.sb("qf2", [128, 16, 64], F32)
    rb2 = S.sb("rb2", [128, 16, 64], F32)
    assert S.mark() <= wAz[1].hi
    S.reset(off_L)
    h1T = S.sb("h1T", [128, 8, 512], BF16)
    assert S.mark() <= off_U
    h1A, h1B = vbuf(h1T.ap()), vbuf(h1T.ap())

    def h1_of(col0, ncols=128):
        out = []
        if col0 < 256:
            out.append(h1A)
        if col0 + ncols > 256:
            out.append(h1B)
        return out
    qf = S.sb("qf", [128, 16, 64], F32)
    rb = S.sb("rb", [128, 16, 64], F32)
    qn = [S.sb(f"qn{i}", [128, 1024], BF16) for i in range(2)]
    kf = [S.sb(f"kf{i}", [128, 4, 64], F32) for i in range(2)]
    ksq = S.sb("ksq", [128, 256], F32)
    krb = S.sb("krb", [128, 4, 64], F32)
    ksq2 = S.sb("ksq2", [128, 256], F32, at=wAz[0].lo + 14336)
    krb2 = S.sb("krb2", [128, 4, 64], F32, at=wAz[0].lo + 15360)
    kn = [S.sb(f"kn{i}", [128, 256], BF16) for i in range(4)]
    vf = [S.sb(f"vf{i}", [128, 256], F32) for i in range(2)]
    KTc = S.sb("KTc", [64, 4, 256], BF16)
    VV = S.sb("VV", [128, 4, 4, 192], BF16)
    PT = [S.sb(f"PT{i}", [128, 512], BF16) for i in range(3)]
    zs1T = S.sb("zs1T", [128, 8, 256], BF16, at=gate.lo)
    off_gT = gate.lo + 4096
    gT = S.sb("gT", [128, 8, 256], BF16, at=off_gT)

    Lp = [S.sb(f"Lp{i}", [128, 2, 128], F32) for i in range(3)]
    ropeT = S.sb("ropeT", [128, 4, 2, 64], F32)
    ropeW = [S.sb("ropeW0", [128, 4, 2, 64], F32), ropeT]
    assert S.mark() <= xin[0].lo, (S.mark(), xin[0].lo)
    c1 = {"qt": 0, "kf": 0, "vf": 0, "pt": 0, "ol": 0, "kn": 0, "lp": 0}

    S.op("pool", lambda e: e.memset(mhalf[:, 8:9], -0.5), reads=[hT_A, hT_B, hT_C],
         writes=[hT, dftP, dftC, h1A, h1B])
    ps_ = prep_stages(1, [XP[0], XP[1]], 0, h1T, [0, 128], h1_of)
    noop_ = lambda: None
    ps1_ = prep_stages(1, [XP[2], XP[3]], 0, h1T, [256, 384], h1_of)
    l0_out(XL, 1, 4, side=[ps_[0], ps_[1], ps_[2], ps_[3], ps1_[0], ps1_[1], ps1_[2], ps1_[3]])

    ring["g"] = [2, 3, 4]
    dma("sp", ropeT.ap(), d_rope.rearrange("(t p) a d -> p t a d", p=128), writes=[ropeT])
    dma("sp", masks.ap(), d_masks, writes=[masks])
    S.op("pool", lambda e: e.memset(VV[:, :, :, 64:128], 0.0), writes=[VV])

    def rope_tables():
      S.op("pool", lambda e: e.memset(VVc[:, :, :, 64:128], 0.0), writes=[VVc])
      for wi in range(2):
          rw = ropeW[wi]
          S.op("pool", lambda e, rw=rw, wi=wi: e.tensor_tensor(
              out=rw[:, :, 0, :], in0=ropeT[:, :, 0, :], in1=qkw[:, wi:wi + 1, :].to_broadcast([128, 4, 64]), op=ALU.mult),
               reads=[ropeT, qkw], writes=[rw])
          S.op("pool", lambda e, rw=rw, wi=wi: e.tensor_tensor(
              out=rw[:, :, 1, 0:32], in0=ropeT[:, :, 1, 0:32], in1=qkw[:, wi:wi + 1, 32:64].to_broadcast([128, 4, 32]),
              op=ALU.mult), reads=[ropeT, qkw], writes=[rw])
          S.op("pool", lambda e, rw=rw, wi=wi: e.tensor_tensor(
              out=rw[:, :, 1, 32:64], in0=ropeT[:, :, 1, 32:64], in1=qkw[:, wi:wi + 1, 0:32].to_broadcast([128, 4, 32]),
              op=ALU.mult), reads=[ropeT, qkw], writes=[rw])

    def head_norm(src_banks, nh, dst, sq=None):
        ssc, rsc = next_ss(nh)
        if sq is None:
            sq = rb if nh == 16 else ksq
        h0 = 0
        for (bank, c0, n) in src_banks:
            sqv = (sq.ap().rearrange("p a d -> p (a d)") if nh == 16 else sq.ap())[:, h0 * 64:(h0 + n) * 64]
            S.op("act", lambda e, bank=bank, c0=c0, n=n, sqv=sqv: e.activation(out=sqv, in_=bank[:, c0:c0 + n * 64],
                                                                               func=AF.Square),
                 reads=[bank], writes=[sq])
            h0 += n
        sq3 = sq.ap() if nh == 16 else sq.ap().rearrange("p (a d) -> p a d", d=64)
        S.op("dve", lambda e: e.tensor_reduce(out=ssc.ap(), in_=sq3, axis=AX.X, op=ALU.add), reads=[sq], writes=[ssc])
        S.op("act", lambda e: e.activation(out=rsc.ap(), in_=ssc.ap(), func=AF.Ln, scale=1.0 / 64, bias=mhalf[:, 12:13]),
             reads=[ssc, mhalf], writes=[rsc])
        S.op("act", lambda e: e.activation(out=rsc.ap(), in_=rsc.ap(), func=AF.Exp, scale=-0.5), reads=[rsc], writes=[rsc])
        h0 = 0
        for (bank, c0, n) in src_banks:
            S.op("dve", lambda e, bank=bank, c0=c0, n=n, h0=h0: e.tensor_tensor(
                out=dst[:, h0:h0 + n, :], in0=bank[:, c0:c0 + n * 64].rearrange("p (a d) -> p a d", d=64),
                in1=rsc[:, h0:h0 + n].unsqueeze(2).to_broadcast([128, n, 64]), op=ALU.mult),
                 reads=[bank, rsc], writes=[dst])
            h0 += n

    def rope_ops(x, tmpb, nh, tile, out_ap, out_buf, wi, eng="dve", hr=None):
        rw = ropeW[wi]
        if hr is None:
            xs, ts, os_, xa, ta, oa, n = x, tmpb, out_buf, x.ap(), tmpb.ap(), out_ap, nh
        else:
            h0, h1 = hr
            n = h1 - h0
            xs = Buf(x.space, x.lo + h0 * 256, x.lo + h1 * 256, x.apv, x.esz)
            ts = Buf(tmpb.space, tmpb.lo + h0 * 256, tmpb.lo + h1 * 256, tmpb.apv, tmpb.esz)
            os_ = Buf(out_buf.space, out_buf.lo + h0 * 128, out_buf.lo + h1 * 128, out_buf.apv, out_buf.esz)
            xa, ta, oa = x[:, h0:h1, :], tmpb[:, h0:h1, :], out_ap[:, h0:h1, :]
        CC = rw[:, tile, 0:1, :].to_broadcast([128, n, 64])
        S1 = rw[:, tile, 1:2, 0:32].to_broadcast([128, n, 32])
        S2 = rw[:, tile, 1:2, 32:64].to_broadcast([128, n, 32])
        S.op(eng, lambda e: e.tensor_tensor(out=ta[:, :, 0:32], in0=xa[:, :, 32:64], in1=S1, op=ALU.mult),
             reads=[xs, rw], writes=[ts])
        S.op(eng, lambda e: e.tensor_tensor(out=ta[:, :, 32:64], in0=xa[:, :, 0:32], in1=S2, op=ALU.mult),
             reads=[xs, rw], writes=[ts])
        S.op(eng, lambda e: e.tensor_tensor(out=xa, in0=xa, in1=CC, op=ALU.mult),
             reads=[xs, rw], writes=[xs])
        S.op(eng, lambda e: e.tensor_tensor(out=oa, in0=xa, in1=ta, op=ALU.add),
             reads=[xs, ts], writes=[os_])

    def vv_fill(dst, t, src_ap, reads):
        dv = dst[:, t, :, :].rearrange("p g (three d) -> p g three d", three=3)
        S.op("dve", lambda e: e.tensor_copy(out=dv[:, :, 0, :], in_=src_ap), reads=reads, writes=[dst])
        S.op("act", lambda e: e.activation(out=dv[:, :, 2, :], in_=src_ap, func=AF.Identity), reads=reads, writes=[dst])

    def l1_front(tiles, r, qtiles, is_sample, o_k, o_v, kv_row0, mid_hook=None, cb=0, late_hook=None):
        nt = len(tiles)
        tr_eng = None if is_sample else "dve"
        ring["g"] = [2, 3, 4, 5, 6, 7]
        ring["gi"] = 0
        kns = {}
        qts = {}

        def k_part(t):
            kvb = nb()
            for kc in range(8):
                mm(kvb, kvb[:, 0:512], h1T[:, kc, cb + t * 128:cb + (t + 1) * 128], wCkv[:, kc, :], kc == 0, kc == 7,
                   h1_of(cb + t * 128) + [wCkv])
            kfi = kf[c1["kf"] % 2]
            c1["kf"] += 1
            kni = kn[c1["kn"] % 4]
            c1["kn"] += 1
            kns[t] = kni
            head_norm([(kvb, 0, 4)], 4, kfi, sq=(ksq if t % 2 == 0 else ksq2))
            knv = kni.ap().rearrange("p (a d) -> p a d", d=64)
            if is_sample:
                rope_ops(kfi, krb if t % 2 == 0 else krb2, 4, t, knv, kni, 1, eng=("dve" if t % 2 == 0 else "pool"))
            else:
                S.op("dve", lambda e, kfi=kfi: e.tensor_tensor(out=kfi.ap(), in0=kfi.ap(),
                                                               in1=qkw[:, 1:2, :].to_broadcast([128, 4, 64]), op=ALU.mult),
                     reads=[kfi, qkw], writes=[kfi])
                S.op("act", lambda e, kfi=kfi, knv=knv: e.activation(out=knv, in_=kfi.ap(), func=AF.Identity),
                     reads=[kfi], writes=[kni])
                dma("sp", o_k[kv_row0 + t * 128:kv_row0 + (t + 1) * 128, :],
                    kfi.ap().rearrange("p a d -> p (a d)"), reads=[kfi], key=f"nk{c1['kf'] % 2}")
                vfi = vf[c1["vf"] % 2]
                c1["vf"] += 1
                S.op("act", lambda e, vfi=vfi, kvb=kvb: e.activation(out=vfi.ap(), in_=kvb[:, 256:512], func=AF.Identity),
                     reads=[kvb], writes=[vfi])
                dma("sp", o_v[kv_row0 + t * 128:kv_row0 + (t + 1) * 128, :], vfi.ap(), reads=[vfi],
                    key=f"nv{c1['vf'] % 2}")
            vv_fill(VV, t, kvb[:, 256:512].rearrange("p (g d) -> p g d", d=64), [kvb])

        def q_part(qi):
            t = qtiles[qi]
            qbanks = [nb(), nb()]
            for n in range(2):
                for kc in range(8):
                    mm(qbanks[n], qbanks[n][:, 0:512], h1T[:, kc, cb + t * 128:cb + (t + 1) * 128],
                       wCq[n][:, kc, :], kc == 0, kc == 7, h1_of(cb + t * 128) + [wCq[n]])
            qfi, rbi = (qf, rb) if qi == 0 else (qf2, rb2)
            head_norm([(qbanks[0], 0, 8), (qbanks[1], 0, 8)], 16, qfi, sq=rbi)
            qni = qn[c1["qt"] % 2]
            QTi = QT[c1["qt"] % 2]
            c1["qt"] += 1
            qnv = qni.ap().rearrange("p (a d) -> p a d", d=64)
            if is_sample:
                if qi == 0:
                    rope_ops(qfi, rbi, 16, t, qnv, qni, 0, eng="dve")
                else:
                    rope_ops(qfi, rbi, 16, t, qnv, qni, 0, eng="pool", hr=(10, 16))
                    rope_ops(qfi, rbi, 16, t, qnv, qni, 0, eng="dve", hr=(0, 10))
            else:
                S.op("dve", lambda e, qnv=qnv, qfi=qfi: e.tensor_tensor(
                    out=qnv, in0=qfi.ap(), in1=qkw[:, 0:1, :].to_broadcast([128, 16, 64]), op=ALU.mult),
                     reads=[qfi, qkw], writes=[qni])
            qts[qi] = (qni, QTi)

        def z_part(e_lo=0, e_hi=8):
            q0 = qtiles[0] * 128
            nq = len(qtiles) * 128
            for e_ in range(e_lo, e_hi):
                zb = nb()
                for kc in range(8):
                    mm(zb, zb[:, 0:nq], wCz[:, kc, e_ * 128:(e_ + 1) * 128], h1T[:, kc, cb + q0:cb + q0 + nq],
                       kc == 0, kc == 7, [wCz] + h1_of(cb + q0, nq))
                S.op("act", lambda e, zb=zb, e_=e_: e.activation(out=zs1T[:, e_, 0:nq], in_=zb[:, 0:nq], func=AF.Silu),
                     reads=[zb], writes=[zs1T])

        def k_tr(t):
            tb = ntb()
            tbv = tb.ap().bitcast(BF16)
            for g in range(4):
                tr(tb, tbv[0:64, g * 128:(g + 1) * 128], kns[t][:, g * 64:(g + 1) * 64], identb.ap(), [kns[t], identb])
            evac(KT[:, :, t * 128:(t + 1) * 128], tbv[0:64, 0:512].rearrange("p (g q) -> p g q", g=4), [tb], [KT],
                 eng=tr_eng)

        def q_tr(qi):
            qni, QTi = qts[qi]
            for half in range(2):
                tb = ntb()
                tbv = tb.ap().bitcast(BF16)
                for hh in range(8):
                    slot = half * 8 + hh
                    g, s_ = slot // 4, slot % 4
                    h = 4 * g + 2 * (s_ % 2) + (s_ // 2)
                    tr(tb, tbv[0:64, hh * 128:(hh + 1) * 128], qni[:, h * 64:(h + 1) * 64], identb.ap(), [qni, identb])
                evac(QTi[:, half * 8:(half + 1) * 8, :], tbv[0:64, 0:1024].rearrange("p (a q) -> p a q", a=8),
                     [tb], [QTi], eng=tr_eng)

        early_k = list(range(nt)) if not (is_sample and SPLIT_SAMPLE_FRONT) else [0, 1, 2]
        early_q = list(range(len(qtiles))) if not (is_sample and SPLIT_SAMPLE_FRONT) else [0]
        for t in early_k:
            k_part(t)
        for qi in early_q:
            q_part(qi)
        if mid_hook:
            mid_hook()
        z_part(0, 4)
        for t in early_k:
            k_tr(t)
        for qi in early_q:
            q_tr(qi)
        if late_hook:
            late_hook()
        z_part(4, 8)
        ring["g"] = [2, 3, 4]
        late = {}
        if is_sample and SPLIT_SAMPLE_FRONT:
            late = {1: (lambda: k_part(3)), 4: (lambda: q_part(1)), 12: (lambda: k_tr(3)), 15: (lambda: q_tr(1))}
            QT_second = QT[c1["qt"] % 2]
            return [qts[0][1], QT_second], late
        return [qts[qi][1] for qi in range(len(qtiles))], late

    def l1_attn(tiles, r, qtiles, is_sample, QTs, o_y, y_row0, side=(), side_at=None):
        side = list(side)
        side_at = dict(side_at or {})
        steps = []
        for qi in range(len(qtiles)):
            if is_sample:
                loc = [(0, 0), (1, None), (2, 2)] if qi == 0 else [(1, 1), (2, None), (3, 3)]
                chunks = [("l", kt, m) for kt, m in loc] + [("c", 0, None), ("c", 1, None)]
            else:
                chunks = [("l", 0, None), ("l", 1, None)]
            for g in range(4):
                for ci, ch in enumerate(chunks):
                    steps.append((qi, g, ci, len(chunks)) + ch)
        LA = 2
        n_fill = 1
        live = {}
        unit_bank = {}
        pending_out = []

        def emit_out(qi):
            t = qtiles[qi]
            for n in range(2):
                ob = nb()
                for kc in range(8):
                    mm(ob, ob[:, 0:512], gT[:, kc, qi * 128:(qi + 1) * 128], wAu[n][:, kc, :],
                       kc == 0, kc == 7, [gT, wAu[n]])
                residual_update(ob, tiles[t], r, n, l=1)
            dma("sp", o_y[y_row0 + qi * 128:y_row0 + (qi + 1) * 128, :], tiles[t].ap(), reads=[tiles[t]])

        def emit_post(ob, g, qi):
            lp = Lp[c1["lp"] % 3]
            rz = lp
            c1["lp"] += 1
            S.op("dve", lambda e, lp=lp, ob=ob, g=g: e.tensor_tensor(
                out=lp.ap(), in0=ob[:, 256:512].rearrange("p (a q) -> p a q", a=2),
                in1=esinkP[:, g, :].unsqueeze(2).to_broadcast([128, 2, 128]), op=ALU.add),
                 reads=[ob, esinkP], writes=[lp])
            S.op("act", lambda e, lp=lp: e.activation(out=lp.ap(), in_=lp.ap(), func=AF.Ln),
                 reads=[lp], writes=[lp])
            S.op("act", lambda e, lp=lp: e.activation(out=lp.ap(), in_=lp.ap(), func=AF.Exp, scale=-1.0),
                 reads=[lp], writes=[lp])
            S.op("dve", lambda e, lp=lp, rz=rz, g=g, qi=qi: e.tensor_tensor(
                out=rz.ap(), in0=lp.ap(), in1=zs1T[:, 2 * g:2 * g + 2, qi * 128:(qi + 1) * 128], op=ALU.mult),
                 reads=[lp, zs1T], writes=[rz])
            S.op("dve", lambda e, rz=rz, ob=ob, g=g, qi=qi: e.tensor_tensor(
                out=gT[:, 2 * g:2 * g + 2, qi * 128:(qi + 1) * 128],
                in0=ob[:, 0:256].rearrange("p (a q) -> p a q", a=2), in1=rz.ap(), op=ALU.mult),
                 reads=[ob, rz], writes=[gT])
            if g == 3:
                pending_out.append([qi, 7])

        post_q = []
        side_every = max(1, (len(steps) - 2) // max(1, len(side)))
        for k in range(len(steps) + LA):
            if side and k >= 1 and (k - 1) % side_every == 0:
                side.pop(0)()
            if k in side_at:
                side_at.pop(k)()
            if k < len(steps):
                qi, g, ci, nch, kind, kt, m = steps[k]
                QTi = QTs[qi]
                sbk = nb()
                kbuf = KT if kind == "l" else KTc
                for _ in range(n_fill):
                    mm(sbk, sbk[:, 0:256], identb.ap(), wCq[0][:, 0, 0:256], True, True, [identb, wCq[0]])
                mm(sbk, sbk[:, 0:512], kbuf[:, g, kt * 128:(kt + 1) * 128],
                   QTi[:, 4 * g:4 * g + 4, :].rearrange("p a q -> p (a q)"), True, m is None, [kbuf, QTi])
                if m is not None:
                    mm(sbk, sbk[:, 0:512], identb.ap(), masks[:, m, :], False, True, [identb, masks])
                pt = PT[c1["pt"] % 3]
                c1["pt"] += 1
                S.op("act", lambda e, pt=pt, sbk=sbk: e.activation(out=pt.ap(), in_=sbk[:, 0:512], func=AF.Exp,
                                                                   scale=0.125), reads=[sbk], writes=[pt])
                live[k] = pt
            while post_q:
                emit_post(*post_q.pop(0))
            if k >= LA:
                qi, g, ci, nch, kind, kt, m = steps[k - LA]
                pt = live.pop(k - LA)
                if ci == 0:
                    unit_bank[(qi, g)] = PS[5 + (c1["ol"] % 3)]
                    c1["ol"] += 1
                ob = unit_bank[(qi, g)]
                vbuf = VV if kind == "l" else VVc
                last = ci == nch - 1
                mm(ob, ob[:, 0:256], vbuf[:, kt, g, 0:128], pt[:, 0:256], ci == 0, False, [vbuf, pt], sgc=True)
                mm(ob, ob[:, 0:256], vbuf[:, kt, g, 64:192], pt[:, 256:512], False, last, [vbuf, pt], sgc=True)
                mm(ob, ob[:, 256:512], ones3[:, 0:128], pt[:, 0:256], False, False, [ones3, pt], sgc=True)
                mm(ob, ob[:, 256:512], ones3[:, 64:192], pt[:, 256:512], False, last, [ones3, pt], sgc=True)
                if last:
                    post_q.append((ob, g, qi))
            for po in list(pending_out):
                po[1] -= 1
                if po[1] <= 0:
                    emit_out(po[0])
                    pending_out.remove(po)
        while post_q:
            emit_post(*post_q.pop(0))
        for f_ in side:
            f_()
        rest = [po[0] for po in pending_out]
        return (lambda: [emit_out(q_) for q_ in rest])

    def cache_prep():
        for t in range(2):
            tb = ntb()
            tbv = tb.ap().bitcast(BF16)
            for g in range(4):
                tr(tb, tbv[0:64, g * 128:(g + 1) * 128], ckb[:, t, g * 64:(g + 1) * 64], identb.ap(), [ckb, identb])
            evac(KTc[:, :, t * 128:(t + 1) * 128], tbv[0:64, 0:512].rearrange("p (g q) -> p g q", g=4), [tb], [KTc])
            vv_fill(VVc, t, cvb[:, t, :].rearrange("p (g d) -> p g d", d=64), [cvb])

    QTs, _ = l1_front([XP[0], XP[1]], 0, [0, 1], False, o_nk, o_nv, 0,
                      late_hook=(lambda: (rope_tables(), cache_prep())), cb=0)
    side0 = prep_stages(1, [XL[0], XL[1]], 1, h1T, [0, 128], h1_of, dve_prep=True)
    tail = l1_attn([XP[0], XP[1]], 0, [0, 1], False, QTs, o_yp, 0, side=side0)
    QTs, _ = l1_front([XP[2], XP[3]], 0, [0, 1], False, o_nk, o_nv, 256, mid_hook=tail, cb=256)
    side1 = prep_stages(1, [XL[2], XL[3]], 1, h1T, [256, 384], h1_of, dve_prep=True)
    tail = l1_attn([XP[2], XP[3]], 0, [0, 1], False, QTs, o_yp, 256, side=side1)
    QTs, late = l1_front(XL, 1, [1, 2], True, o_nk, o_nv, 0, mid_hook=tail, cb=0)
    tail = l1_attn(XL, 1, [1, 2], True, QTs, o_ys, 0, side_at=late)
    tail()

    S.emit()
    build_program.stats = dict(n_ops=len(S.ops), n_sems=S.n_sems, sb_hw=S.hw)
    return nc


def _host_consts():
    bf = ml_dtypes.bfloat16
    c = {}
    c["identb"] = np.eye(128, dtype=np.float32).astype(bf)
    c["identf"] = np.eye(2, dtype=np.float32)
    sel = np.zeros((2, 2, 128), np.float32)
    sel[0, 0, :] = 1.0
    sel[1, 1, :] = 1.0
    c["sel"] = sel
    i = np.arange(256, dtype=np.float64)
    ang = 2 * np.pi * np.outer(i, i) / 256.0
    c["dftP"] = np.stack([np.cos(ang) / 16.0, np.sin(ang) / 16.0], axis=1).astype(np.float32).astype(bf)
    c["dftC"] = np.stack([np.cos(ang) / 16.0, -np.sin(ang) / 16.0], axis=1).astype(np.float32).astype(bf)
    return c


def _core_layout(ch):
    own = [2 * ch, 2 * ch + 1]
    left = 2 * ch - 1 if ch > 0 else None
    right = 2 * ch + 2 if ch < 3 else None
    used = set(own) | ({left} if left is not None else set()) | ({right} if right is not None else set())
    spare = [t for t in range(8) if t not in used]
    subs = list(spare)
    l_t = left if left is not None else subs.pop()
    r_t = right if right is not None else subs.pop()
    local = [l_t, own[0], own[1], r_t]
    rest = [t for t in range(8) if t not in local]
    return local, rest, left is not None, right is not None


def _core_consts(ch):
    bf = ml_dtypes.bfloat16
    local, rest, has_l, has_r = _core_layout(ch)
    perm = local + rest
    pos_all = np.concatenate([np.arange(t * 128, (t + 1) * 128) for t in perm]).astype(np.float64)
    pos_loc = pos_all[:512]
    ang = 2 * np.pi * np.outer(pos_all, pos_loc) / 1024.0
    dftS = np.stack([np.cos(ang) / 32.0, np.sin(ang) / 32.0], axis=1).astype(np.float32).astype(bf)
    n_freq = 16
    inv = (10000.0 ** (-np.arange(n_freq, dtype=np.float32) / n_freq)).astype(np.float32)
    row = np.floor(pos_loc / 64).astype(np.float32)
    col = (pos_loc % 64).astype(np.float32)
    a = np.concatenate([row[:, None] * inv, col[:, None] * inv], axis=-1).astype(np.float32)
    cs, sn = np.cos(a), np.sin(a)
    rope = np.stack([np.concatenate([cs, cs], -1), np.concatenate([-sn, sn], -1)], axis=1).astype(np.float32)
    j = np.arange(128)[:, None]
    q = np.arange(128)[None, :]
    ge = (j >= q).astype(np.float32)
    le = (j <= q).astype(np.float32)
    valid = np.stack([ge * (1.0 if has_l else 0.0), ge, le, le * (1.0 if has_r else 0.0)], axis=1)
    masks = np.tile((valid - 1.0) * 30000.0, (1, 1, 4)).astype(np.float32).astype(bf)
    return perm, dftS, rope, masks


_CACHE = {}


def kernel(x_prompt, x_sample, cache_k_l1, cache_v_l1, c, c_ctx,
           norm_w_l0, w_mod_l0, b_mod_l0, w_in_l0, w_out_l0,
           norm_w_l1, w_mod_l1, b_mod_l1, w_in_l1, q_norm_w_l1, k_norm_w_l1, sink_l1, w_out_l1):
    f = lambda a: np.ascontiguousarray(np.asarray(a, dtype=np.float32))
    x_prompt, x_sample = f(x_prompt), f(x_sample)
    cache_k_l1, cache_v_l1, c, c_ctx = f(cache_k_l1), f(cache_v_l1), f(c), f(c_ctx)
    if "nc" not in _CACHE:
        _CACHE["nc"] = build_program()
        _CACHE["hc"] = _host_consts()
        _CACHE["cc"] = [_core_consts(ch) for ch in range(4)]
    nc, hc = _CACHE["nc"], _CACHE["hc"]
    nwT = np.ascontiguousarray(np.stack([f(norm_w_l0).reshape(8, 128).T, f(norm_w_l1).reshape(8, 128).T], axis=1))
    qkw = np.concatenate([f(q_norm_w_l1), f(k_norm_w_l1)])
    shared = dict(wmod0=f(w_mod_l0), bmod0=f(b_mod_l0), win0=f(w_in_l0), wout0=f(w_out_l0),
                  wmod1=f(w_mod_l1), bmod1=f(b_mod_l1), win1=f(w_in_l1), wout1=f(w_out_l1),
                  nwT=nwT, qkw=qkw, sink=f(sink_l1), **hc)
    in_maps = []
    for core in range(NCORES):
        b, ch = core // 4, core % 4
        perm, dftS, rope, masks = _CACHE["cc"][ch]
        xs = np.ascontiguousarray(x_sample[b].reshape(8, 128, 1024)[perm].reshape(1024, 1024))
        m = dict(shared)
        m.update(xp=np.ascontiguousarray(x_prompt[2 * core:2 * core + 2].reshape(512, 1024)), xs=xs,
                 condT=np.ascontiguousarray(np.stack([c_ctx, c[b]], axis=1).reshape(8, 128, 2).transpose(1, 0, 2)),
                 ck=np.ascontiguousarray(cache_k_l1[b].reshape(256, 256)),
                 cv=np.ascontiguousarray(cache_v_l1[b].reshape(256, 256)),
                 dftS=dftS, rope=rope, masks=masks)
        in_maps.append(m)
    res = run_bass_kernel_spmd(nc, in_maps, core_ids=list(range(NCORES)))
    R = res.results
    y_prompt = np.concatenate([R[i]["yp"].reshape(2, 256, 1024) for i in range(NCORES)], axis=0)
    y_sample = np.stack([np.concatenate([R[b * 4 + ch]["ys"] for ch in range(4)], axis=0) for b in range(2)], axis=0)
    new_k = np.concatenate([R[i]["nk"].reshape(2, 256, 4, 64) for i in range(NCORES)], axis=0)
    new_v = np.concatenate([R[i]["nv"].reshape(2, 256, 4, 64) for i in range(NCORES)], axis=0)
    return (y_prompt.astype(np.float32), y_sample.astype(np.float32),
            new_k.astype(np.float32), new_v.astype(np.float32))
```

```python
import numpy as np
import ml_dtypes
from contextlib import ExitStack
import concourse.bass as bass
import concourse.mybir as mybir
from concourse.bass_utils import run_bass_kernel_spmd

F32 = mybir.dt.float32
BF16 = mybir.dt.bfloat16
AF = mybir.ActivationFunctionType
ALU = mybir.AluOpType
AX = mybir.AxisListType

SB_BASE = 18432
SB_END = 229376
NCORES = 8
SPLIT_SAMPLE_FRONT = False


class Buf:
    def __init__(self, space, lo, hi, apv, esz):
        self.space, self.lo, self.hi, self.apv, self.esz = space, lo, hi, apv, esz

    def ap(self):
        return self.apv

    def __getitem__(self, k):
        return self.apv[k]

    def col(self, k0, k1):
        return Buf(self.space, self.lo + k0 * self.esz, self.lo + k1 * self.esz,
                   self.apv[:, k0:k1], self.esz)


class Op:
    __slots__ = ("eng", "fn", "reads", "writes", "dma", "deps", "idx", "milestone", "semkey", "cnt")

    def __init__(self, eng, fn, reads, writes, dma, semkey):
        self.eng, self.fn, self.reads, self.writes, self.dma = eng, fn, reads, writes, dma
        self.deps = set()
        self.idx = None
        self.milestone = False
        self.semkey = semkey
        self.cnt = None


class Sched:
    ENGS = ("pe", "act", "dve", "pool", "sp")

    def __init__(self, nc):
        self.nc = nc
        self.ops = []
        self.sb_ptr = SB_BASE
        self.hw = SB_BASE
        self.state = {}
        self.nbuf = 0
        self.nkey = 0

    def sb(self, name, shape, dtype, at=None):
        esz = 2 if dtype == BF16 else 4
        n = int(np.prod(shape[1:])) * esz
        n_al = (n + 31) // 32 * 32
        if at is None:
            off = self.sb_ptr
            self.sb_ptr += n_al
        else:
            off = at
        assert off + n <= SB_END, (name, off, n, SB_END)
        self.hw = max(self.hw, off + n)
        self.nbuf += 1
        h = self.nc.alloc_sbuf_tensor_at(f"{name}_{self.nbuf}", list(shape), dtype, offset=off)
        return Buf("sb", off, off + n, h.ap(), esz)

    def mark(self):
        return self.sb_ptr

    def reset(self, m):
        self.sb_ptr = m

    def op(self, eng, fn, reads=(), writes=(), dma=False, semkey=None):
        if dma and semkey is None:
            self.nkey += 1
            semkey = f"k{self.nkey}"
        o = Op(eng, fn, list(reads), list(writes), dma, semkey)
        st_items = list(self.state.items())
        for b in o.reads:
            for key, st in st_items:
                if key[0] == b.space and key[1] < b.hi and b.lo < key[2]:
                    if st[0] is not None:
                        o.deps.add(st[0])
        for b in o.writes:
            for key, st in st_items:
                if key[0] == b.space and key[1] < b.hi and b.lo < key[2]:
                    if st[0] is not None:
                        o.deps.add(st[0])
                    for r in st[1]:
                        o.deps.add(r)
        o.deps.discard(o)
        for b in o.reads:
            key = (b.space, b.lo, b.hi)
            st = self.state.setdefault(key, [None, []])
            if not o.dma:
                st[1] = [r for r in st[1] if r.dma or r.eng != o.eng]
            st[1].append(o)
        for b in o.writes:
            key = (b.space, b.lo, b.hi)
            for k2 in list(self.state.keys()):
                if k2 != key and k2[0] == b.space and b.lo <= k2[1] and k2[2] <= b.hi:
                    del self.state[k2]
            self.state[key] = [o, []]
        self.ops.append(o)
        return o

    def emit(self):
        nc = self.nc
        for o in self.ops:
            for d in o.deps:
                d.milestone = True
        cnt = {e: 0 for e in self.ENGS}
        dcnt = {}
        for o in self.ops:
            if o.dma:
                dcnt[o.semkey] = dcnt.get(o.semkey, 0) + 1
                o.cnt = dcnt[o.semkey]
            elif o.milestone:
                cnt[o.eng] += 1
                o.idx = cnt[o.eng]
        sem_names = [("c_" + e) for e in self.ENGS if e != "sp"] + ["d_" + str(k) for k in dcnt]
        self.n_sems = len(sem_names)
        sems = {}
        with ExitStack() as es:
            for n in sem_names:
                sems[n] = es.enter_context(nc.semaphore(n))
            block = es.enter_context(nc.Block())
            per = {e: [o for o in self.ops if o.eng == e] for e in self.ENGS}
            final_waits = [("d_" + str(k), 16 * v) for k, v in dcnt.items()]

            def run(eng_name, eng):
                known = {}
                for o in per[eng_name]:
                    need = {}
                    for d in o.deps:
                        if d.dma:
                            s, v = "d_" + str(d.semkey), 16 * d.cnt
                        else:
                            if d.eng == "pe" and eng_name == "pe" and not o.dma:
                                continue
                            s, v = "c_" + d.eng, d.idx
                        if need.get(s, 0) < v:
                            need[s] = v
                    for s, v in need.items():
                        if known.get(s, 0) < v:
                            eng.wait_ge(sems[s], v)
                            known[s] = v
                    ins = o.fn(eng)
                    if o.dma:
                        ins.then_inc(sems["d_" + str(o.semkey)], 16)
                    elif o.milestone:
                        ins.then_inc(sems["c_" + eng_name], 1)
                if eng_name == "sp":
                    for s, v in final_waits:
                        if known.get(s, 0) < v:
                            eng.wait_ge(sems[s], v)

            @block.sync
            def _(e):
                run("sp", e)

            @block.tensor
            def _(e):
                run("pe", e)

            @block.scalar
            def _(e):
                run("act", e)

            @block.vector
            def _(e):
                run("dve", e)

            @block.gpsimd
            def _(e):
                run("pool", e)


def build_program():
    nc = bass.Bass("TRN2", target_bir_lowering=False)
    S = Sched(nc)

    def din(name, shape, dt=F32):
        return nc.dram_tensor(name, list(shape), dt, kind="ExternalInput").ap()

    def dout(name, shape):
        return nc.dram_tensor(name, list(shape), F32, kind="ExternalOutput").ap()

    d_xp = din("xp", [512, 1024])
    d_xs = din("xs", [1024, 1024])
    d_condT = din("condT", [128, 8, 2])
    d_ck = din("ck", [256, 256])
    d_cv = din("cv", [256, 256])
    d_wmod = [din("wmod0", [1024, 3072]), din("wmod1", [1024, 3072])]
    d_bmod = [din("bmod0", [3072]), din("bmod1", [3072])]
    d_win0 = din("win0", [1024, 2048])
    d_wout0 = din("wout0", [1024, 1024])
    d_win1 = din("win1", [1024, 2560])
    d_wout1 = din("wout1", [1024, 1024])
    d_nwT = din("nwT", [128, 2, 8])
    d_qkw = din("qkw", [128])
    d_sink = din("sink", [16])
    d_identb = din("identb", [128, 128], BF16)
    d_identf = din("identf", [2, 2])
    d_sel = din("sel", [2, 2, 128])
    d_dftP = din("dftP", [256, 2, 256], BF16)
    d_dftC = din("dftC", [256, 2, 256], BF16)
    d_dftS = din("dftS", [1024, 2, 512], BF16)
    d_rope = din("rope", [512, 2, 64])
    d_masks = din("masks", [128, 4, 512], BF16)
    o_yp = dout("yp", [512, 1024])
    o_ys = dout("ys", [256, 1024])
    o_nk = dout("nk", [512, 256])
    o_nv = dout("nv", [512, 256])

    PS = []
    for i in range(8):
        h = nc.alloc_psum_tensor(f"ps{i}", [128, 512], F32)
        PS.append(Buf("ps", i * 2048, (i + 1) * 2048, h.ap(), 4))
    ring = {"g": [2, 3, 4, 5, 6, 7], "gi": 0, "t": [0, 1], "ti": 0}

    def nb():
        b = PS[ring["g"][ring["gi"] % len(ring["g"])]]
        ring["gi"] += 1
        return b

    def ntb():
        b = PS[ring["t"][ring["ti"] % 2]]
        ring["ti"] += 1
        return b

    XP = [S.sb(f"xp{t}", [128, 1024], F32) for t in range(4)]
    XL = [S.sb(f"xl{t}", [128, 1024], F32) for t in range(4)]
    wAu = [S.sb(f"wAu{i}", [128, 8, 512], BF16) for i in range(2)]
    wAz = [S.sb(f"wAz{i}", [128, 8, 512], BF16) for i in range(2)]
    wCq = [S.sb(f"wCq{i}", [128, 8, 512], BF16) for i in range(2)]
    wCkv = S.sb("wCkv", [128, 8, 512], BF16)
    wm = [S.sb(f"wm{i}", [128, 8, 256], BF16) for i in range(2)]
    bm = [S.sb(f"bm{i}", [2, 256], F32) for i in range(2)]
    mrow = [S.sb(f"mrow{i}", [2, 256], F32) for i in range(2)]
    gate = S.sb("gate", [128, 2, 1024], F32)
    MT = [S.sb(f"mt{l}", [128, 2, 8, 2], F32) for l in range(2)]
    identb = S.sb("identb", [128, 128], BF16)
    identf = S.sb("identf", [2, 2], F32)
    sel = S.sb("sel", [2, 2, 128], F32)
    nwT = S.sb("nwT", [128, 2, 8], F32)
    condT = S.sb("condT", [128, 8, 2], F32)
    scT = S.sb("scT", [128, 8, 2], BF16)
    ssb = S.sb("ssb", [128, 160], F32)
    rsb = S.sb("rsb", [128, 160], F32)
    mhalf = S.sb("mhalf", [128, 16], F32)
    ones3 = S.sb("ones3", [128, 192], BF16)
    qkw = S.sb("qkw", [128, 2, 64], F32)
    esink = S.sb("esink", [128, 16], F32)
    esinkP = S.sb("esinkP", [128, 4, 2], F32)
    off_wB = S.mark()
    wB = S.sb("wB", [128, 8, 1024], BF16)
    off_dftS = S.mark()
    dftS = S.sb("dftS", [128, 8, 2, 512], BF16)
    xn = [S.sb(f"xn{i}", [128, 1024], BF16) for i in range(2)]
    tmp = [S.sb(f"tmp{i}", [128, 512], F32) for i in range(2)]
    off_L = S.mark()
    dftP = S.sb("dftP", [128, 2, 2, 256], BF16)
    dftC = S.sb("dftC", [128, 2, 2, 256], BF16)
    hT = S.sb("hT", [128, 8, 1024], BF16)
    off_U = S.mark()
    U = S.sb("U", [128, 8, 512], BF16)
    AT = S.sb("AT", [128, 4, 2, 512], BF16)
    zs = [S.sb(f"zs{i}", [128, 512], BF16) for i in range(2)]
    yT = S.sb("yT", [128, 8, 512], BF16)
    xin = [S.sb(f"xin{i}", [128, 1024], F32) for i in range(2)]
    end_L0 = S.mark()
    gate1 = S.sb("gate1", [128, 2, 1024], F32, at=xin[0].lo)
    assert xin[1].hi == xin[0].lo + 8192
    gates = [gate, gate1]
    wm0 = wm + [S.sb("wmx0", [128, 8, 256], BF16, at=off_U), S.sb("wmx1", [128, 8, 256], BF16, at=off_U + 4096)]

    cnt = {"ss": 0, "ev": 0, "xn": 0, "tmp": 0, "wm": 0, "zs": 0, "bm": 0, "xin": 0}
    vcnt = {"n": 0}

    def vbuf(apv):
        vcnt["n"] += 1
        return Buf("v", vcnt["n"] * 10, vcnt["n"] * 10 + 1, apv, 2)

    hT_A, hT_B, hT_C = vbuf(hT.ap()), vbuf(hT.ap()), vbuf(hT.ap())

    def hT_of(col0, ncols=128):
        out = []
        if col0 < 256:
            out.append(hT_A)
        if col0 < 512 and col0 + ncols > 256:
            out.append(hT_B)
        if col0 + ncols > 512:
            out.append(hT_C)
        return out

    def dma(q, out_ap, in_ap, reads=(), writes=(), key=None):
        S.op(q, lambda e: e.dma_start(out=out_ap, in_=in_ap), reads=reads, writes=writes, dma=True, semkey=key)

    def mm(bank, out_ap, lhsT, rhs, start, stop, reads, sgc=False):
        if sgc:
            S.op("pe", lambda e: e.matmul(out_ap, lhsT=lhsT, rhs=rhs, start=start, stop=stop, skip_group_check=True),
                 reads=reads, writes=[bank])
        else:
            S.op("pe", lambda e: e.matmul(out_ap, lhsT=lhsT, rhs=rhs, start=start, stop=stop),
                 reads=reads, writes=[bank])

    def tr(bank, out_ap, in_ap, ident_ap, reads):
        S.op("pe", lambda e: e.transpose(out=out_ap, in_=in_ap, identity=ident_ap), reads=reads, writes=[bank])

    def evac(dst_ap, src_ap, reads, writes, eng=None):
        if eng is None:
            eng = "act" if cnt["ev"] % 2 == 0 else "dve"
            cnt["ev"] += 1
        if eng == "act":
            S.op("act", lambda e: e.activation(out=dst_ap, in_=src_ap, func=AF.Identity), reads=reads, writes=writes)
        else:
            S.op(eng, lambda e: e.tensor_copy(out=dst_ap, in_=src_ap), reads=reads, writes=writes)

    def next_ss(n=1):
        k = cnt["ss"]
        cnt["ss"] += n
        assert cnt["ss"] <= 160
        return ssb.col(k, k + n), rsb.col(k, k + n)

    def rstd_ops(ssc, rsc, n, inv):
        S.op("dve", lambda e: e.tensor_scalar(out=rsc.ap(), in0=ssc.ap(), scalar1=inv, scalar2=1e-6,
                                              op0=ALU.mult, op1=ALU.add), reads=[ssc], writes=[rsc])
        S.op("pool", lambda e: e.tensor_tensor(out=rsc.ap(), in0=rsc.ap(), in1=mhalf[:, 0:n], op=ALU.pow),
             reads=[rsc, mhalf], writes=[rsc])

    dma("sp", identb.ap(), d_identb, writes=[identb])
    dma("sp", identf.ap(), d_identf, writes=[identf])
    dma("sp", sel.ap(), d_sel, writes=[sel])
    dma("sp", nwT.ap(), d_nwT, writes=[nwT])
    dma("sp", condT.ap(), d_condT, writes=[condT])
    dma("sp", qkw.ap(), d_qkw.partition_broadcast(128).rearrange("p (a d) -> p a d", a=2), writes=[qkw])
    dma("sp", esink.ap(), d_sink.partition_broadcast(128), writes=[esink])
    S.op("pool", lambda e: e.memset(ssb.ap(), 0.0), writes=[ssb])
    S.op("pool", lambda e: e.memset(mhalf.ap(), -0.5), writes=[mhalf])
    S.op("pool", lambda e: e.memset(mhalf[:, 12:13], 1e-6), writes=[mhalf])
    S.op("pool", lambda e: e.memset(ones3.ap(), 1.0), writes=[ones3])
    S.op("pool", lambda e: e.memset(ones3[:, 64:128], 0.0), writes=[ones3])
    S.op("act", lambda e: e.activation(out=esink.ap(), in_=esink.ap(), func=AF.Exp), reads=[esink], writes=[esink])
    esv = esink.ap().rearrange("p (g a two) -> p g a two", g=4, a=2)
    S.op("dve", lambda e: e.tensor_copy(out=esinkP[0:64, :, :], in_=esv[0:64, :, :, 0]), reads=[esink], writes=[esinkP])
    S.op("dve", lambda e: e.tensor_copy(out=esinkP[64:128, :, :], in_=esv[64:128, :, :, 1]), reads=[esink],
         writes=[esinkP])
    S.op("act", lambda e: e.activation(out=scT.ap(), in_=condT.ap(), func=AF.Silu), reads=[condT], writes=[scT])
    for t in range(4):
        dma("sp", XP[t].ap(), d_xp[t * 128:(t + 1) * 128, :], writes=[XP[t]])

    def deferred_loads():
        for t in range(4):
            dma("sp", XL[t].ap(), d_xs[t * 128:(t + 1) * 128, :], reads=[wAz[1]], writes=[XL[t]])
        dma("sp", dftS.ap(), d_dftS.rearrange("(t p) f n -> p t f n", p=128), reads=[wAz[1]], writes=[dftS])

    mod_state = {}

    def mod_load(l, j, bufs):
        i = cnt["wm"] % len(bufs)
        cnt["wm"] += 1
        ib = cnt["bm"] % 2
        cnt["bm"] += 1
        dma("pool", bufs[i].ap(), d_wmod[l][:, j * 256:(j + 1) * 256].rearrange("(kc p) n -> p kc n", p=128),
            writes=[bufs[i]], key=f"wm{i}")
        mod_state[(l, j)] = (bufs[i], ib)

    def mod_a(l, j):
        wbuf, ib = mod_state[(l, j)]
        dma("sp", bm[ib].ap(), d_bmod[l][j * 256:(j + 1) * 256].partition_broadcast(2), writes=[bm[ib]],
            key=f"bm{ib}")
        bank = nb()
        for kc in range(8):
            mm(bank, bank[0:2, 0:256], scT[:, kc, :], wbuf[:, kc, :], kc == 0, kc == 7, [scT, wbuf])
        S.op("dve", lambda e: e.tensor_tensor(out=mrow[ib].ap(), in0=bank[0:2, 0:256], in1=bm[ib].ap(), op=ALU.add),
             reads=[bank, bm[ib]], writes=[mrow[ib]])

    def mod_b(l, j):
        wbuf, ib = mod_state[(l, j)]
        which = j // 4
        if which < 2:
            b2 = nb()
            for h in range(2):
                tr(b2, b2[:, h * 2:(h + 1) * 2], mrow[ib][:, h * 128:(h + 1) * 128], identf.ap(), [mrow[ib], identf])
            kc0 = (j % 4) * 2
            src = b2[:, 0:4].rearrange("p (h r) -> p h r", h=2)
            if which == 0:
                S.op("dve", lambda e: e.tensor_copy(out=MT[l][:, 0, kc0:kc0 + 2, :], in_=src),
                     reads=[b2], writes=[MT[l]])
            else:
                nwb = nwT[:, l, kc0:kc0 + 2].unsqueeze(2).to_broadcast([128, 2, 2])
                S.op("dve", lambda e: e.scalar_tensor_tensor(out=MT[l][:, 1, kc0:kc0 + 2, :], in0=src, scalar=1.0,
                                                             in1=nwb, op0=ALU.add, op1=ALU.mult),
                     reads=[b2, nwT], writes=[MT[l]])
        else:
            for r in range(2):
                b3 = nb()
                mm(b3, b3[:, 0:256], sel[:, r, :], mrow[ib].ap(), True, True, [sel, mrow[ib]])
                c0 = (j - 8) * 256
                evac(gates[l][:, r, c0:c0 + 256], b3[:, 0:256], [b3], [gates[l]], eng="act")

    def prep_A(xbuf, on_dve=False):
        ssc, rsc = next_ss()
        xi = xn[cnt["xn"] % 2]
        cnt["xn"] += 1
        if on_dve:
            S.op("dve", lambda e: e.scalar_tensor_tensor(out=xi.ap(), in0=xbuf.ap(), scalar=1.0, in1=xbuf.ap(),
                                                         op0=ALU.mult, op1=ALU.mult, accum_out=ssc.ap()),
                 reads=[xbuf], writes=[xi, ssc])
            rstd_ops(ssc, rsc, 1, 1.0 / 1024)
            S.op("dve", lambda e: e.tensor_scalar(out=xi.ap(), in0=xbuf.ap(), scalar1=rsc.ap(), scalar2=None,
                                                  op0=ALU.mult), reads=[xbuf, rsc], writes=[xi])
            return xi
        S.op("act", lambda e: e.activation(out=xi.ap(), in_=xbuf.ap(), func=AF.Square, accum_out=ssc.ap()),
             reads=[xbuf], writes=[xi, ssc])
        rstd_ops(ssc, rsc, 1, 1.0 / 1024)
        S.op("act", lambda e: e.activation(out=xi.ap(), in_=xbuf.ap(), func=AF.Identity, scale=rsc.ap()),
             reads=[xbuf, rsc], writes=[xi])
        return xi

    def prep_B(l, xi, r, hbuf, col0, hdeps=None, shift_eng="dve"):
        hdeps = hdeps if hdeps is not None else [hbuf]
        tb = ntb()
        tbv = tb.ap().bitcast(BF16)
        for kc in range(8):
            tr(tb, tbv[:, kc * 128:(kc + 1) * 128], xi[:, kc * 128:(kc + 1) * 128], identb.ap(), [xi, identb])
        hv = hbuf[:, :, col0:col0 + 128]
        scb = MT[l][:, 1, :, r:r + 1].to_broadcast([128, 8, 128])
        shb = MT[l][:, 0, :, r:r + 1].to_broadcast([128, 8, 128])
        S.op("dve", lambda e: e.tensor_tensor(out=hv, in0=tbv.rearrange("p (k q) -> p k q", k=8), in1=scb, op=ALU.mult),
             reads=[tb, MT[l]], writes=hdeps)
        S.op(shift_eng, lambda e: e.tensor_tensor(out=hv, in0=hv, in1=shb, op=ALU.add), reads=hdeps + [MT[l]],
             writes=hdeps)

    def prep_seq(l, tiles, r, hbuf, rest_src=None, pre=None, cols=None, depf=None, tidx=None):
        nt = len(tiles)
        xis = list(pre) if pre else []
        for step in range(nt + 1):
            if step < nt and step >= len(xis):
                xb = tiles[step]
                if xb is None:
                    xb = xin[cnt["xin"] % 2]
                    dma("sp", xb.ap(), rest_src(tidx[step]), writes=[xb], key=f"xin{cnt['xin'] % 2}")
                    cnt["xin"] += 1
                xis.append(prep_A(xb))
            if step >= 1:
                c0 = cols[step - 1] if cols else (step - 1) * 128
                prep_B(l, xis[step - 1], r, hbuf, c0, depf(c0) if depf else None)

    def prep_stages(l, tiles, r, hbuf, cols, depf, rest_src=None, tidx=None, dve_prep=False):
        st = []
        xis = {}
        nt = len(tiles)

        def mk(step):
            def f():
                if step >= 2:
                    prep_B(l, xis[step - 2], r, hbuf, cols[step - 2], depf(cols[step - 2]),
                           shift_eng=("pool" if (l == 1 and dve_prep) else "dve"))
                if step < nt:
                    xb = tiles[step]
                    if xb is None:
                        xb = xin[cnt["xin"] % 2]
                        dma("sp", xb.ap(), rest_src(tidx[step]), writes=[xb], key=f"xin{cnt['xin'] % 2}")
                        cnt["xin"] += 1
                    xis[step] = prep_A(xb, on_dve=(l == 1 and dve_prep and step % 2 == 0))
            return f

        for step in range(nt + 2):
            st.append(mk(step))
        return st

    def residual_update(bank, xbuf, r, n, l=0):
        tp = tmp[cnt["tmp"] % 2]
        cnt["tmp"] += 1
        gl = gates[l]
        S.op("dve", lambda e: e.tensor_tensor(out=tp.ap(), in0=bank[:, 0:512], in1=gl[:, r, n * 512:(n + 1) * 512],
                                              op=ALU.mult), reads=[bank, gl], writes=[tp])
        S.op("pool", lambda e: e.tensor_tensor(out=xbuf[:, n * 512:(n + 1) * 512], in0=xbuf[:, n * 512:(n + 1) * 512],
                                               in1=tp.ap(), op=ALU.add), reads=[xbuf, tp], writes=[xbuf])

    def l0_main(nt, n_local, dft_ap, dft_buf, hooks=None, cb=0, stage_q=None):
        ncols = n_local * 128
        stage_q = list(stage_q or [])
        n_stage = len(stage_q)
        n_points = 2 * (nt + (4 if ncols == 256 else 8) + 4)
        pt = {"i": 0, "done": 0}

        def point():
            pt["i"] += 1
            target = (pt["i"] * n_stage) // n_points
            while pt["done"] < target and stage_q:
                stage_q.pop(0)()
                pt["done"] += 1
        for gp in range(2):
            for t in range(nt):
                bank = nb()
                for kc in range(8):
                    mm(bank, bank[:, 0:512], hT[:, kc, cb + t * 128:cb + (t + 1) * 128],
                       wAu[gp][:, kc, :], kc == 0, kc == 7, hT_of(cb + t * 128) + [wAu[gp]])
                evac(U[:, t, :], bank[:, 0:512], [bank], [U])
                point()
            for cc in range(4):
                if ncols == 256:
                    bank = nb()
                    for f in range(2):
                        for t in range(nt):
                            mm(bank, bank[:, f * 256:(f + 1) * 256], U[:, t, cc * 128:(cc + 1) * 128], dft_ap(t, f),
                               f == 0 and t == 0, t == nt - 1, [U, dft_buf], sgc=True)
                    evac(AT[:, cc, :, 0:256], bank[:, 0:512].rearrange("p (f q) -> p f q", f=2), [bank], [AT])
                    point()
                    continue
                for f in range(2):
                    bank = nb()
                    for t in range(nt):
                        mm(bank, bank[:, 0:ncols], U[:, t, cc * 128:(cc + 1) * 128], dft_ap(t, f),
                           t == 0, t == nt - 1, [U, dft_buf])
                    evac(AT[:, cc, f, 0:ncols], bank[:, 0:ncols], [bank], [AT])
                    point()
            if hooks and ("afterA", gp) in hooks:
                hooks[("afterA", gp)]()
            for e2 in range(4):
                e_ = gp * 4 + e2
                g, cp = e_ // 2, e_ % 2
                zb = nb()
                for kc in range(8):
                    mm(zb, zb[:, 0:ncols], wAz[gp][:, kc, e2 * 128:(e2 + 1) * 128], hT[:, kc, cb:cb + ncols],
                       kc == 0, kc == 7, [wAz[gp]] + hT_of(cb, ncols))
                zi = zs[cnt["zs"] % 2]
                cnt["zs"] += 1
                S.op("act", lambda e, zb=zb, zi=zi: e.activation(out=zi[:, 0:ncols], in_=zb[:, 0:ncols], func=AF.Silu),
                     reads=[zb], writes=[zi])
                yb = nb()
                k = 0
                for cch in range(2):
                    for f in range(2):
                        mm(yb, yb[:, 0:ncols], dftC[:, cch, f, cp * 128:(cp + 1) * 128],
                           AT[:, (g - gp * 2) * 2 + cch, f, 0:ncols], k == 0, k == 3, [dftC, AT])
                        k += 1
                S.op("dve", lambda e, yb=yb, zi=zi, e_=e_: e.tensor_tensor(out=yT[:, e_, 0:ncols], in0=yb[:, 0:ncols],
                                                                           in1=zi[:, 0:ncols], op=ALU.mult),
                     reads=[yb, zi], writes=[yT])
                if hooks and ("afterE", e_) in hooks:
                    hooks[("afterE", e_)]()
                point()
        while stage_q:
            stage_q.pop(0)()
        if hooks and "afterZ" in hooks:
            hooks["afterZ"]()

    def l0_out(tiles, r, n_local, side=()):
        side = list(side)
        n_side, n_grp, gi_, done_ = len(side), 2 * n_local, 0, 0
        for t in range(n_local):
            for n in range(2):
                gi_ += 1
                while side and done_ < -(-(gi_ * n_side) // n_grp):
                    side.pop(0)()
                    done_ += 1
                ob = nb()
                for kc in range(8):
                    mm(ob, ob[:, 0:512], yT[:, kc, t * 128:(t + 1) * 128], wB[:, kc, n * 512:(n + 1) * 512],
                       kc == 0, kc == 7, [yT, wB])
                residual_update(ob, tiles[t], r, n)
        for f_ in side:
            f_()

    for j in range(4):
        mod_load(0, j, wm0)
    pre0 = [prep_A(XP[0]), prep_A(XP[1])]
    for j in range(8):
        mod_a(0, j)
        mod_b(0, j)
        if j + 4 < 10:
            mod_load(0, j + 4, wm0)
        if j == 3:
            dma("pool", wAu[0].ap(), d_win0[:, 0:512].rearrange("(kc p) n -> p kc n", p=128), writes=[wAu[0]])
        if j == 2:
            dma("sp", dftP.ap(), d_dftP.rearrange("(t p) f n -> p t f n", p=128), writes=[dftP])
            dma("sp", dftC.ap(), d_dftC.rearrange("(t p) f n -> p t f n", p=128), writes=[dftC])
    dma("pool", wAz[0].ap(), d_win0[:, 1024:1536].rearrange("(kc p) n -> p kc n", p=128), reads=[MT[0]], writes=[wAz[0]])
    dma("pool", wAu[1].ap(), d_win0[:, 512:1024].rearrange("(kc p) n -> p kc n", p=128), reads=[MT[0]], writes=[wAu[1]])

    def late_weights(which):
        if which == 0:
            dma("pool", wAz[1].ap(), d_win0[:, 1536:2048].rearrange("(kc p) n -> p kc n", p=128), reads=[wAz[0]],
                writes=[wAz[1]])
        else:
            dma("pool", wB.ap(), d_wout0.rearrange("(kc p) n -> p kc n", p=128), reads=[wAu[1]], writes=[wB])

    rest_src = lambda t: d_xs[t * 128:(t + 1) * 128, :]
    prep_seq(0, [XP[0], XP[1]], 0, hT, pre=pre0, cols=[0, 128], depf=hT_of)
    hk0 = {}
    st1_ = prep_stages(0, [XP[2], XP[3]], 0, hT, [256, 384], hT_of)

    def mk_a(l, j, nxt):
        def hk():
            mod_a(l, j)
            if nxt is not None:
                mod_load(l, nxt, wm)
        return hk

    def mk_b(l, j):
        return lambda: mod_b(l, j)

    q0_ = []
    for j in range(8, 12):
        q0_ += [mk_a(0, j, j + 2 if j + 2 < 12 else None), mk_b(0, j)]
    def mk_both(e_):
        def hk():
            q0_[e_]()
            if e_ == 1:
                late_weights(0)
            if e_ == 2:
                late_weights(1)
            if e_ in (1, 3, 5, 7):
                st1_[(e_ - 1) // 2]()
        return hk

    for e_ in range(8):
        hk0[("afterE", e_)] = mk_both(e_)
    l0_main(2, 2, lambda t, f: dftP[:, t, f, :], dftP, cb=0, hooks=hk0)
    cnt["wm"] = 0
    q1_ = []
    for j in range(12):
        q1_ += [mk_a(1, j, j + 2 if j + 2 < 12 else None), mk_b(1, j)]

    def mk_multi(fs):
        return lambda: [f() for f in fs]

    l0_out([XP[0], XP[1]], 0, 2,
           side=prep_stages(0, [None] * 4, 1, hT, [512, 640, 768, 896], hT_of, rest_src=rest_src, tidx=[4, 5, 6, 7]))
    deferred_loads()
    l0_main(2, 2, lambda t, f: dftP[:, t, f, :], dftP, cb=256)
    mod_load(1, 0, wm)
    mod_load(1, 1, wm)
    l0_out([XP[2], XP[3]], 0, 2, side=prep_stages(0, XL, 1, hT, [0, 128, 256, 384], hT_of))

    hooks = {}
    sq_ = list(q1_)
    sq_.insert(4, lambda: dma("pool", wCkv.ap(), d_win1[:, 1024:1536].rearrange("(kc p) n -> p kc n", p=128),
                              writes=[wCkv]))
    sq_.insert(11, lambda: dma("pool", wCq[0].ap(), d_win1[:, 0:512].rearrange("(kc p) n -> p kc n", p=128),
                               writes=[wCq[0]]))
    sq_.insert(18, lambda: dma("pool", wCq[1].ap(), d_win1[:, 512:1024].rearrange("(kc p) n -> p kc n", p=128),
                               writes=[wCq[1]]))
    wCz_holder = {}

    def load_wCz():
        wCz_holder["b"] = S.sb("wCz", [128, 8, 1024], BF16, at=off_dftS)
        dma("pool", wCz_holder["b"].ap(), d_win1[:, 1536:2560].rearrange("(kc p) n -> p kc n", p=128),
            writes=[wCz_holder["b"]])

    hooks[("afterA", 1)] = load_wCz
    ckb = S.sb("ckb", [128, 2, 256], BF16, at=wAz[0].lo + 12288)
    cvb = S.sb("cvb", [128, 2, 256], BF16, at=wAz[0].lo + 13312)

    def after_z():
        dma("pool", ckb.ap(), d_ck.rearrange("(t p) n -> p t n", p=128), writes=[ckb])
        dma("pool", cvb.ap(), d_cv.rearrange("(t p) n -> p t n", p=128), writes=[cvb])
        for n_ in range(2):
            dma("pool", wAu[n_].ap(), d_wout1[:, n_ * 512:(n_ + 1) * 512].rearrange("(kc p) n -> p kc n", p=128),
                writes=[wAu[n_]])

    hooks["afterZ"] = after_z
    l0_main(8, 4, lambda t, f: dftS[:, t, f, :], dftS, hooks=hooks, stage_q=sq_)
    wCz = wCz_holder["b"]

    S.reset(off_wB)
    QT = [S.sb(f"QT{i}", [64, 16, 128], BF16) for i in range(2)]
    KT = S.sb("KT", [64, 4, 512], BF16)
    VVc = S.sb("VVc", [128, 2, 4, 192], BF16)
    assert S.mark() <= off_dftS
    S.reset(wAz[0].lo)
    masks = S.sb("masks", [128, 4, 512], BF16)
    qf2 = S.sb("qf2", [128, 16, 64], F32)
    rb2 = S.sb("rb2", [128, 16, 64], F32)
    assert S.mark() <= wAz[1].hi
    S.reset(off_L)
    h1T = S.sb("h1T", [128, 8, 512], BF16)
    assert S.mark() <= off_U
    h1A, h1B = vbuf(h1T.ap()), vbuf(h1T.ap())

    def h1_of(col0, ncols=128):
        out = []
        if col0 < 256:
            out.append(h1A)
        if col0 + ncols > 256:
            out.append(h1B)
        return out
    qf = S.sb("qf", [128, 16, 64], F32)
    rb = S.sb("rb", [128, 16, 64], F32)
    qn = [S.sb(f"qn{i}", [128, 1024], BF16) for i in range(2)]
    kf = [S.sb(f"kf{i}", [128, 4, 64], F32) for i in range(2)]
    ksq = S.sb("ksq", [128, 256], F32)
    krb = S.sb("krb", [128, 4, 64], F32)
    ksq2 = S.sb("ksq2", [128, 256], F32, at=wAz[0].lo + 14336)
    krb2 = S.sb("krb2", [128, 4, 64], F32, at=wAz[0].lo + 15360)
    kn = [S.sb(f"kn{i}", [128, 256], BF16) for i in range(4)]
    vf = [S.sb(f"vf{i}", [128, 256], F32) for i in range(2)]
    KTc = S.sb("KTc", [64, 4, 256], BF16)
    VV = S.sb("VV", [128, 4, 4, 192], BF16)
    PT = [S.sb(f"PT{i}", [128, 512], BF16) for i in range(3)]
    zs1T = S.sb("zs1T", [128, 8, 256], BF16, at=gate.lo)
    off_gT = gate.lo + 4096
    gT = S.sb("gT", [128, 8, 256], BF16, at=off_gT)

    Lp = [S.sb(f"Lp{i}", [128, 2, 128], F32) for i in range(3)]
    ropeT = S.sb("ropeT", [128, 4, 2, 64], F32)
    ropeW = [S.sb("ropeW0", [128, 4, 2, 64], F32), ropeT]
    assert S.mark() <= xin[0].lo, (S.mark(), xin[0].lo)
    c1 = {"qt": 0, "kf": 0, "vf": 0, "pt": 0, "ol": 0, "kn": 0, "lp": 0}

    S.op("pool", lambda e: e.memset(mhalf[:, 8:9], -0.5), reads=[hT_A, hT_B, hT_C],
         writes=[hT, dftP, dftC, h1A, h1B])
    ps_ = prep_stages(1, [XP[0], XP[1]], 0, h1T, [0, 128], h1_of)
    noop_ = lambda: None
    ps1_ = prep_stages(1, [XP[2], XP[3]], 0, h1T, [256, 384], h1_of)
    l0_out(XL, 1, 4, side=[ps_[0], ps_[1], ps_[2], ps_[3], ps1_[0], ps1_[1], ps1_[2], ps1_[3]])

    ring["g"] = [2, 3, 4]
    dma("sp", ropeT.ap(), d_rope.rearrange("(t p) a d -> p t a d", p=128), writes=[ropeT])
    dma("sp", masks.ap(), d_masks, writes=[masks])
    S.op("pool", lambda e: e.memset(VV[:, :, :, 64:128], 0.0), writes=[VV])

    def rope_tables():
      S.op("pool", lambda e: e.memset(VVc[:, :, :, 64:128], 0.0), writes=[VVc])
      for wi in range(2):
          rw = ropeW[wi]
          S.op("pool", lambda e, rw=rw, wi=wi: e.tensor_tensor(
              out=rw[:, :, 0, :], in0=ropeT[:, :, 0, :], in1=qkw[:, wi:wi + 1, :].to_broadcast([128, 4, 64]), op=ALU.mult),
               reads=[ropeT, qkw], writes=[rw])
          S.op("pool", lambda e, rw=rw, wi=wi: e.tensor_tensor(
              out=rw[:, :, 1, 0:32], in0=ropeT[:, :, 1, 0:32], in1=qkw[:, wi:wi + 1, 32:64].to_broadcast([128, 4, 32]),
              op=ALU.mult), reads=[ropeT, qkw], writes=[rw])
          S.op("pool", lambda e, rw=rw, wi=wi: e.tensor_tensor(
              out=rw[:, :, 1, 32:64], in0=ropeT[:, :, 1, 32:64], in1=qkw[:, wi:wi + 1, 0:32].to_broadcast([128, 4, 32]),
              op=ALU.mult), reads=[ropeT, qkw], writes=[rw])

    def head_norm(src_banks, nh, dst, sq=None):
        ssc, rsc = next_ss(nh)
        if sq is None:
            sq = rb if nh == 16 else ksq
        h0 = 0
        for (bank, c0, n) in src_banks:
            sqv = (sq.ap().rearrange("p a d -> p (a d)") if nh == 16 else sq.ap())[:, h0 * 64:(h0 + n) * 64]
            S.op("act", lambda e, bank=bank, c0=c0, n=n, sqv=sqv: e.activation(out=sqv, in_=bank[:, c0:c0 + n * 64],
                                                                               func=AF.Square),
                 reads=[bank], writes=[sq])
            h0 += n
        sq3 = sq.ap() if nh == 16 else sq.ap().rearrange("p (a d) -> p a d", d=64)
        S.op("dve", lambda e: e.tensor_reduce(out=ssc.ap(), in_=sq3, axis=AX.X, op=ALU.add), reads=[sq], writes=[ssc])
        S.op("act", lambda e: e.activation(out=rsc.ap(), in_=ssc.ap(), func=AF.Ln, scale=1.0 / 64, bias=mhalf[:, 12:13]),
             reads=[ssc, mhalf], writes=[rsc])
        S.op("act", lambda e: e.activation(out=rsc.ap(), in_=rsc.ap(), func=AF.Exp, scale=-0.5), reads=[rsc], writes=[rsc])
        h0 = 0
        for (bank, c0, n) in src_banks:
            S.op("dve", lambda e, bank=bank, c0=c0, n=n, h0=h0: e.tensor_tensor(
                out=dst[:, h0:h0 + n, :], in0=bank[:, c0:c0 + n * 64].rearrange("p (a d) -> p a d", d=64),
                in1=rsc[:, h0:h0 + n].unsqueeze(2).to_broadcast([128, n, 64]), op=ALU.mult),
                 reads=[bank, rsc], writes=[dst])
            h0 += n

    def rope_ops(x, tmpb, nh, tile, out_ap, out_buf, wi, eng="dve", hr=None):
        rw = ropeW[wi]
        if hr is None:
            xs, ts, os_, xa, ta, oa, n = x, tmpb, out_buf, x.ap(), tmpb.ap(), out_ap, nh
        else:
            h0, h1 = hr
            n = h1 - h0
            xs = Buf(x.space, x.lo + h0 * 256, x.lo + h1 * 256, x.apv, x.esz)
            ts = Buf(tmpb.space, tmpb.lo + h0 * 256, tmpb.lo + h1 * 256, tmpb.apv, tmpb.esz)
            os_ = Buf(out_buf.space, out_buf.lo + h0 * 128, out_buf.lo + h1 * 128, out_buf.apv, out_buf.esz)
            xa, ta, oa = x[:, h0:h1, :], tmpb[:, h0:h1, :], out_ap[:, h0:h1, :]
        CC = rw[:, tile, 0:1, :].to_broadcast([128, n, 64])
        S1 = rw[:, tile, 1:2, 0:32].to_broadcast([128, n, 32])
        S2 = rw[:, tile, 1:2, 32:64].to_broadcast([128, n, 32])
        S.op(eng, lambda e: e.tensor_tensor(out=ta[:, :, 0:32], in0=xa[:, :, 32:64], in1=S1, op=ALU.mult),
             reads=[xs, rw], writes=[ts])
        S.op(eng, lambda e: e.tensor_tensor(out=ta[:, :, 32:64], in0=xa[:, :, 0:32], in1=S2, op=ALU.mult),
             reads=[xs, rw], writes=[ts])
        S.op(eng, lambda e: e.tensor_tensor(out=xa, in0=xa, in1=CC, op=ALU.mult),
             reads=[xs, rw], writes=[xs])
        S.op(eng, lambda e: e.tensor_tensor(out=oa, in0=xa, in1=ta, op=ALU.add),
             reads=[xs, ts], writes=[os_])

    def vv_fill(dst, t, src_ap, reads):
        dv = dst[:, t, :, :].rearrange("p g (three d) -> p g three d", three=3)
        S.op("dve", lambda e: e.tensor_copy(out=dv[:, :, 0, :], in_=src_ap), reads=reads, writes=[dst])
        S.op("act", lambda e: e.activation(out=dv[:, :, 2, :], in_=src_ap, func=AF.Identity), reads=reads, writes=[dst])

    def l1_front(tiles, r, qtiles, is_sample, o_k, o_v, kv_row0, mid_hook=None, cb=0, late_hook=None):
        nt = len(tiles)
        tr_eng = None if is_sample else "dve"
        ring["g"] = [2, 3, 4, 5, 6, 7]
        ring["gi"] = 0
        kns = {}
        qts = {}

        def k_part(t):
            kvb = nb()
            for kc in range(8):
                mm(kvb, kvb[:, 0:512], h1T[:, kc, cb + t * 128:cb + (t + 1) * 128], wCkv[:, kc, :], kc == 0, kc == 7,
                   h1_of(cb + t * 128) + [wCkv])
            kfi = kf[c1["kf"] % 2]
            c1["kf"] += 1
            kni = kn[c1["kn"] % 4]
            c1["kn"] += 1
            kns[t] = kni
            head_norm([(kvb, 0, 4)], 4, kfi, sq=(ksq if t % 2 == 0 else ksq2))
            knv = kni.ap().rearrange("p (a d) -> p a d", d=64)
            if is_sample:
                rope_ops(kfi, krb if t % 2 == 0 else krb2, 4, t, knv, kni, 1, eng=("dve" if t % 2 == 0 else "pool"))
            else:
                S.op("dve", lambda e, kfi=kfi: e.tensor_tensor(out=kfi.ap(), in0=kfi.ap(),
                                                               in1=qkw[:, 1:2, :].to_broadcast([128, 4, 64]), op=ALU.mult),
                     reads=[kfi, qkw], writes=[kfi])
                S.op("act", lambda e, kfi=kfi, knv=knv: e.activation(out=knv, in_=kfi.ap(), func=AF.Identity),
                     reads=[kfi], writes=[kni])
                dma("sp", o_k[kv_row0 + t * 128:kv_row0 + (t + 1) * 128, :],
                    kfi.ap().rearrange("p a d -> p (a d)"), reads=[kfi], key=f"nk{c1['kf'] % 2}")
                vfi = vf[c1["vf"] % 2]
                c1["vf"] += 1
                S.op("act", lambda e, vfi=vfi, kvb=kvb: e.activation(out=vfi.ap(), in_=kvb[:, 256:512], func=AF.Identity),
                     reads=[kvb], writes=[vfi])
                dma("sp", o_v[kv_row0 + t * 128:kv_row0 + (t + 1) * 128, :], vfi.ap(), reads=[vfi],
                    key=f"nv{c1['vf'] % 2}")
            vv_fill(VV, t, kvb[:, 256:512].rearrange("p (g d) -> p g d", d=64), [kvb])

        def q_part(qi):
            t = qtiles[qi]
            qbanks = [nb(), nb()]
            for n in range(2):
                for kc in range(8):
                    mm(qbanks[n], qbanks[n][:, 0:512], h1T[:, kc, cb + t * 128:cb + (t + 1) * 128],
                       wCq[n][:, kc, :], kc == 0, kc == 7, h1_of(cb + t * 128) + [wCq[n]])
            qfi, rbi = (qf, rb) if qi == 0 else (qf2, rb2)
            head_norm([(qbanks[0], 0, 8), (qbanks[1], 0, 8)], 16, qfi, sq=rbi)
            qni = qn[c1["qt"] % 2]
            QTi = QT[c1["qt"] % 2]
            c1["qt"] += 1
            qnv = qni.ap().rearrange("p (a d) -> p a d", d=64)
            if is_sample:
                if qi == 0:
                    rope_ops(qfi, rbi, 16, t, qnv, qni, 0, eng="dve")
                else:
                    rope_ops(qfi, rbi, 16, t, qnv, qni, 0, eng="pool", hr=(10, 16))
                    rope_ops(qfi, rbi, 16, t, qnv, qni, 0, eng="dve", hr=(0, 10))
            else:
                S.op("dve", lambda e, qnv=qnv, qfi=qfi: e.tensor_tensor(
                    out=qnv, in0=qfi.ap(), in1=qkw[:, 0:1, :].to_broadcast([128, 16, 64]), op=ALU.mult),
                     reads=[qfi, qkw], writes=[qni])
            qts[qi] = (qni, QTi)

        def z_part(e_lo=0, e_hi=8):
            q0 = qtiles[0] * 128
            nq = len(qtiles) * 128
            for e_ in range(e_lo, e_hi):
                zb = nb()
                for kc in range(8):
                    mm(zb, zb[:, 0:nq], wCz[:, kc, e_ * 128:(e_ + 1) * 128], h1T[:, kc, cb + q0:cb + q0 + nq],
                       kc == 0, kc == 7, [wCz] + h1_of(cb + q0, nq))
                S.op("act", lambda e, zb=zb, e_=e_: e.activation(out=zs1T[:, e_, 0:nq], in_=zb[:, 0:nq], func=AF.Silu),
                     reads=[zb], writes=[zs1T])

        def k_tr(t):
            tb = ntb()
            tbv = tb.ap().bitcast(BF16)
            for g in range(4):
                tr(tb, tbv[0:64, g * 128:(g + 1) * 128], kns[t][:, g * 64:(g + 1) * 64], identb.ap(), [kns[t], identb])
            evac(KT[:, :, t * 128:(t + 1) * 128], tbv[0:64, 0:512].rearrange("p (g q) -> p g q", g=4), [tb], [KT],
                 eng=tr_eng)

        def q_tr(qi):
            qni, QTi = qts[qi]
            for half in range(2):
                tb = ntb()
                tbv = tb.ap().bitcast(BF16)
                for hh in range(8):
                    slot = half * 8 + hh
                    g, s_ = slot // 4, slot % 4
                    h = 4 * g + 2 * (s_ % 2) + (s_ // 2)
                    tr(tb, tbv[0:64, hh * 128:(hh + 1) * 128], qni[:, h * 64:(h + 1) * 64], identb.ap(), [qni, identb])
                evac(QTi[:, half * 8:(half + 1) * 8, :], tbv[0:64, 0:1024].rearrange("p (a q) -> p a q", a=8),
                     [tb], [QTi], eng=tr_eng)

        early_k = list(range(nt)) if not (is_sample and SPLIT_SAMPLE_FRONT) else [0, 1, 2]
        early_q = list(range(len(qtiles))) if not (is_sample and SPLIT_SAMPLE_FRONT) else [0]
        for t in early_k:
            k_part(t)
        for qi in early_q:
            q_part(qi)
        if mid_hook:
            mid_hook()
        z_part(0, 4)
        for t in early_k:
            k_tr(t)
        for qi in early_q:
            q_tr(qi)
        if late_hook:
            late_hook()
        z_part(4, 8)
        ring["g"] = [2, 3, 4]
        late = {}
        if is_sample and SPLIT_SAMPLE_FRONT:
            late = {1: (lambda: k_part(3)), 4: (lambda: q_part(1)), 12: (lambda: k_tr(3)), 15: (lambda: q_tr(1))}
            QT_second = QT[c1["qt"] % 2]
            return [qts[0][1], QT_second], late
        return [qts[qi][1] for qi in range(len(qtiles))], late

    def l1_attn(tiles, r, qtiles, is_sample, QTs, o_y, y_row0, side=(), side_at=None):
        side = list(side)
        side_at = dict(side_at or {})
        steps = []
        for qi in range(len(qtiles)):
            if is_sample:
                loc = [(0, 0), (1, None), (2, 2)] if qi == 0 else [(1, 1), (2, None), (3, 3)]
                chunks = [("l", kt, m) for kt, m in loc] + [("c", 0, None), ("c", 1, None)]
            else:
                chunks = [("l", 0, None), ("l", 1, None)]
            for g in range(4):
                for ci, ch in enumerate(chunks):
                    steps.append((qi, g, ci, len(chunks)) + ch)
        LA = 2
        n_fill = 0 if is_sample else 1
        live = {}
        unit_bank = {}
        pending_out = []

        def emit_out(qi):
            t = qtiles[qi]
            for n in range(2):
                ob = nb()
                for kc in range(8):
                    mm(ob, ob[:, 0:512], gT[:, kc, qi * 128:(qi + 1) * 128], wAu[n][:, kc, :],
                       kc == 0, kc == 7, [gT, wAu[n]])
                residual_update(ob, tiles[t], r, n, l=1)
            dma("sp", o_y[y_row0 + qi * 128:y_row0 + (qi + 1) * 128, :], tiles[t].ap(), reads=[tiles[t]])

        def emit_post(ob, g, qi):
            lp = Lp[c1["lp"] % 3]
            rz = lp
            c1["lp"] += 1
            S.op("dve", lambda e, lp=lp, ob=ob, g=g: e.tensor_tensor(
                out=lp.ap(), in0=ob[:, 256:512].rearrange("p (a q) -> p a q", a=2),
                in1=esinkP[:, g, :].unsqueeze(2).to_broadcast([128, 2, 128]), op=ALU.add),
                 reads=[ob, esinkP], writes=[lp])
            S.op("act", lambda e, lp=lp: e.activation(out=lp.ap(), in_=lp.ap(), func=AF.Ln),
                 reads=[lp], writes=[lp])
            S.op("act", lambda e, lp=lp: e.activation(out=lp.ap(), in_=lp.ap(), func=AF.Exp, scale=-1.0),
                 reads=[lp], writes=[lp])
            S.op("dve", lambda e, lp=lp, rz=rz, g=g, qi=qi: e.tensor_tensor(
                out=rz.ap(), in0=lp.ap(), in1=zs1T[:, 2 * g:2 * g + 2, qi * 128:(qi + 1) * 128], op=ALU.mult),
                 reads=[lp, zs1T], writes=[rz])
            S.op("dve", lambda e, rz=rz, ob=ob, g=g, qi=qi: e.tensor_tensor(
                out=gT[:, 2 * g:2 * g + 2, qi * 128:(qi + 1) * 128],
                in0=ob[:, 0:256].rearrange("p (a q) -> p a q", a=2), in1=rz.ap(), op=ALU.mult),
                 reads=[ob, rz], writes=[gT])
            if g == 3:
                pending_out.append([qi, 7])

        post_q = []
        side_every = max(1, (len(steps) - 2) // max(1, len(side)))
        for k in range(len(steps) + LA):
            if side and k >= 1 and (k - 1) % side_every == 0:
                side.pop(0)()
            if k in side_at:
                side_at.pop(k)()
            if k < len(steps):
                qi, g, ci, nch, kind, kt, m = steps[k]
                QTi = QTs[qi]
                sbk = nb()
                kbuf = KT if kind == "l" else KTc
                for _ in range(n_fill):
                    mm(sbk, sbk[:, 0:256], identb.ap(), wCq[0][:, 0, 0:256], True, True, [identb, wCq[0]])
                mm(sbk, sbk[:, 0:512], kbuf[:, g, kt * 128:(kt + 1) * 128],
                   QTi[:, 4 * g:4 * g + 4, :].rearrange("p a q -> p (a q)"), True, m is None, [kbuf, QTi])
                if m is not None:
                    mm(sbk, sbk[:, 0:512], identb.ap(), masks[:, m, :], False, True, [identb, masks])
                pt = PT[c1["pt"] % 3]
                c1["pt"] += 1
                S.op("act", lambda e, pt=pt, sbk=sbk: e.activation(out=pt.ap(), in_=sbk[:, 0:512], func=AF.Exp,
                                                                   scale=0.125), reads=[sbk], writes=[pt])
                live[k] = pt
            while post_q:
                emit_post(*post_q.pop(0))
            if k >= LA:
                qi, g, ci, nch, kind, kt, m = steps[k - LA]
                pt = live.pop(k - LA)
                if ci == 0:
                    unit_bank[(qi, g)] = PS[5 + (c1["ol"] % 3)]
                    c1["ol"] += 1
                ob = unit_bank[(qi, g)]
                vbuf = VV if kind == "l" else VVc
                last = ci == nch - 1
                mm(ob, ob[:, 0:256], vbuf[:, kt, g, 0:128], pt[:, 0:256], ci == 0, False, [vbuf, pt], sgc=True)
                mm(ob, ob[:, 0:256], vbuf[:, kt, g, 64:192], pt[:, 256:512], False, last, [vbuf, pt], sgc=True)
                mm(ob, ob[:, 256:512], ones3[:, 0:128], pt[:, 0:256], False, False, [ones3, pt], sgc=True)
                mm(ob, ob[:, 256:512], ones3[:, 64:192], pt[:, 256:512], False, last, [ones3, pt], sgc=True)
                if last:
                    post_q.append((ob, g, qi))
            for po in list(pending_out):
                po[1] -= 1
                if po[1] <= 0:
                    emit_out(po[0])
                    pending_out.remove(po)
        while post_q:
            emit_post(*post_q.pop(0))
        for f_ in side:
            f_()
        rest = [po[0] for po in pending_out]
        return (lambda: [emit_out(q_) for q_ in rest])

    def cache_prep():
        for t in range(2):
            tb = ntb()
            tbv = tb.ap().bitcast(BF16)
            for g in range(4):
                tr(tb, tbv[0:64, g * 128:(g + 1) * 128], ckb[:, t, g * 64:(g + 1) * 64], identb.ap(), [ckb, identb])
            evac(KTc[:, :, t * 128:(t + 1) * 128], tbv[0:64, 0:512].rearrange("p (g q) -> p g q", g=4), [tb], [KTc])
            vv_fill(VVc, t, cvb[:, t, :].rearrange("p (g d) -> p g d", d=64), [cvb])

    QTs, _ = l1_front([XP[0], XP[1]], 0, [0, 1], False, o_nk, o_nv, 0,
                      late_hook=(lambda: (rope_tables(), cache_prep())), cb=0)
    side0 = prep_stages(1, [XL[0], XL[1]], 1, h1T, [0, 128], h1_of, dve_prep=True)
    tail = l1_attn([XP[0], XP[1]], 0, [0, 1], False, QTs, o_yp, 0, side=side0)
    QTs, _ = l1_front([XP[2], XP[3]], 0, [0, 1], False, o_nk, o_nv, 256, mid_hook=tail, cb=256)
    side1 = prep_stages(1, [XL[2], XL[3]], 1, h1T, [256, 384], h1_of, dve_prep=True)
    tail = l1_attn([XP[2], XP[3]], 0, [0, 1], False, QTs, o_yp, 256, side=side1)
    QTs, late = l1_front(XL, 1, [1, 2], True, o_nk, o_nv, 0, mid_hook=tail, cb=0)
    tail = l1_attn(XL, 1, [1, 2], True, QTs, o_ys, 0, side_at=late)
    tail()

    S.emit()
    build_program.stats = dict(n_ops=len(S.ops), n_sems=S.n_sems, sb_hw=S.hw)
    return nc


def _host_consts():
    bf = ml_dtypes.bfloat16
    c = {}
    c["identb"] = np.eye(128, dtype=np.float32).astype(bf)
    c["identf"] = np.eye(2, dtype=np.float32)
    sel = np.zeros((2, 2, 128), np.float32)
    sel[0, 0, :] = 1.0
    sel[1, 1, :] = 1.0
    c["sel"] = sel
    i = np.arange(256, dtype=np.float64)
    ang = 2 * np.pi * np.outer(i, i) / 256.0
    c["dftP"] = np.stack([np.cos(ang) / 16.0, np.sin(ang) / 16.0], axis=1).astype(np.float32).astype(bf)
    c["dftC"] = np.stack([np.cos(ang) / 16.0, -np.sin(ang) / 16.0], axis=1).astype(np.float32).astype(bf)
    return c


def _core_layout(ch):
    own = [2 * ch, 2 * ch + 1]
    left = 2 * ch - 1 if ch > 0 else None
    right = 2 * ch + 2 if ch < 3 else None
    used = set(own) | ({left} if left is not None else set()) | ({right} if right is not None else set())
    spare = [t for t in range(8) if t not in used]
    subs = list(spare)
    l_t = left if left is not None else subs.pop()
    r_t = right if right is not None else subs.pop()
    local = [l_t, own[0], own[1], r_t]
    rest = [t for t in range(8) if t not in local]
    return local, rest, left is not None, right is not None


def _core_consts(ch):
    bf = ml_dtypes.bfloat16
    local, rest, has_l, has_r = _core_layout(ch)
    perm = local + rest
    pos_all = np.concatenate([np.arange(t * 128, (t + 1) * 128) for t in perm]).astype(np.float64)
    pos_loc = pos_all[:512]
    ang = 2 * np.pi * np.outer(pos_all, pos_loc) / 1024.0
    dftS = np.stack([np.cos(ang) / 32.0, np.sin(ang) / 32.0], axis=1).astype(np.float32).astype(bf)
    n_freq = 16
    inv = (10000.0 ** (-np.arange(n_freq, dtype=np.float32) / n_freq)).astype(np.float32)
    row = np.floor(pos_loc / 64).astype(np.float32)
    col = (pos_loc % 64).astype(np.float32)
    a = np.concatenate([row[:, None] * inv, col[:, None] * inv], axis=-1).astype(np.float32)
    cs, sn = np.cos(a), np.sin(a)
    rope = np.stack([np.concatenate([cs, cs], -1), np.concatenate([-sn, sn], -1)], axis=1).astype(np.float32)
    j = np.arange(128)[:, None]
    q = np.arange(128)[None, :]
    ge = (j >= q).astype(np.float32)
    le = (j <= q).astype(np.float32)
    valid = np.stack([ge * (1.0 if has_l else 0.0), ge, le, le * (1.0 if has_r else 0.0)], axis=1)
    masks = np.tile((valid - 1.0) * 30000.0, (1, 1, 4)).astype(np.float32).astype(bf)
    return perm, dftS, rope, masks


_CACHE = {}


def kernel(x_prompt, x_sample, cache_k_l1, cache_v_l1, c, c_ctx,
           norm_w_l0, w_mod_l0, b_mod_l0, w_in_l0, w_out_l0,
           norm_w_l1, w_mod_l1, b_mod_l1, w_in_l1, q_norm_w_l1, k_norm_w_l1, sink_l1, w_out_l1):
    f = lambda a: np.ascontiguousarray(np.asarray(a, dtype=np.float32))
    x_prompt, x_sample = f(x_prompt), f(x_sample)
    cache_k_l1, cache_v_l1, c, c_ctx = f(cache_k_l1), f(cache_v_l1), f(c), f(c_ctx)
    if "nc" not in _CACHE:
        _CACHE["nc"] = build_program()
        _CACHE["hc"] = _host_consts()
        _CACHE["cc"] = [_core_consts(ch) for ch in range(4)]
    nc, hc = _CACHE["nc"], _CACHE["hc"]
    nwT = np.ascontiguousarray(np.stack([f(norm_w_l0).reshape(8, 128).T, f(norm_w_l1).reshape(8, 128).T], axis=1))
    qkw = np.concatenate([f(q_norm_w_l1), f(k_norm_w_l1)])
    shared = dict(wmod0=f(w_mod_l0), bmod0=f(b_mod_l0), win0=f(w_in_l0), wout0=f(w_out_l0),
                  wmod1=f(w_mod_l1), bmod1=f(b_mod_l1), win1=f(w_in_l1), wout1=f(w_out_l1),
                  nwT=nwT, qkw=qkw, sink=f(sink_l1), **hc)
    in_maps = []
    for core in range(NCORES):
        b, ch = core // 4, core % 4
        perm, dftS, rope, masks = _CACHE["cc"][ch]
        xs = np.ascontiguousarray(x_sample[b].reshape(8, 128, 1024)[perm].reshape(1024, 1024))
        m = dict(shared)
        m.update(xp=np.ascontiguousarray(x_prompt[2 * core:2 * core + 2].reshape(512, 1024)), xs=xs,
                 condT=np.ascontiguousarray(np.stack([c_ctx, c[b]], axis=1).reshape(8, 128, 2).transpose(1, 0, 2)),
                 ck=np.ascontiguousarray(cache_k_l1[b].reshape(256, 256)),
                 cv=np.ascontiguousarray(cache_v_l1[b].reshape(256, 256)),
                 dftS=dftS, rope=rope, masks=masks)
        in_maps.append(m)
    res = run_bass_kernel_spmd(nc, in_maps, core_ids=list(range(NCORES)))
    R = res.results
    y_prompt = np.concatenate([R[i]["yp"].reshape(2, 256, 1024) for i in range(NCORES)], axis=0)
    y_sample = np.stack([np.concatenate([R[b * 4 + ch]["ys"] for ch in range(4)], axis=0) for b in range(2)], axis=0)
    new_k = np.concatenate([R[i]["nk"].reshape(2, 256, 4, 64) for i in range(NCORES)], axis=0)
    new_v = np.concatenate([R[i]["nv"].reshape(2, 256, 4, 64) for i in range(NCORES)], axis=0)
    return (y_prompt.astype(np.float32), y_sample.astype(np.float32),
            new_k.astype(np.float32), new_v.astype(np.float32))
```

```python
import numpy as np
import ml_dtypes
from contextlib import ExitStack
import concourse.bass as bass
import concourse.mybir as mybir
from concourse.bass_utils import run_bass_kernel_spmd

F32 = mybir.dt.float32
BF16 = mybir.dt.bfloat16
AF = mybir.ActivationFunctionType
ALU = mybir.AluOpType
AX = mybir.AxisListType

SB_BASE = 18432
SB_END = 229376
NCORES = 8
SPLIT_SAMPLE_FRONT = False


class Buf:
    def __init__(self, space, lo, hi, apv, esz):
        self.space, self.lo, self.hi, self.apv, self.esz = space, lo, hi, apv, esz

    def ap(self):
        return self.apv

    def __getitem__(self, k):
        return self.apv[k]

    def col(self, k0, k1):
        return Buf(self.space, self.lo + k0 * self.esz, self.lo + k1 * self.esz,
                   self.apv[:, k0:k1], self.esz)


class Op:
    __slots__ = ("eng", "fn", "reads", "writes", "dma", "deps", "idx", "milestone", "semkey", "cnt")

    def __init__(self, eng, fn, reads, writes, dma, semkey):
        self.eng, self.fn, self.reads, self.writes, self.dma = eng, fn, reads, writes, dma
        self.deps = set()
        self.idx = None
        self.milestone = False
        self.semkey = semkey
        self.cnt = None


class Sched:
    ENGS = ("pe", "act", "dve", "pool", "sp")

    def __init__(self, nc):
        self.nc = nc
        self.ops = []
        self.sb_ptr = SB_BASE
        self.hw = SB_BASE
        self.state = {}
        self.nbuf = 0
        self.nkey = 0

    def sb(self, name, shape, dtype, at=None):
        esz = 2 if dtype == BF16 else 4
        n = int(np.prod(shape[1:])) * esz
        n_al = (n + 31) // 32 * 32
        if at is None:
            off = self.sb_ptr
            self.sb_ptr += n_al
        else:
            off = at
        assert off + n <= SB_END, (name, off, n, SB_END)
        self.hw = max(self.hw, off + n)
        self.nbuf += 1
        h = self.nc.alloc_sbuf_tensor_at(f"{name}_{self.nbuf}", list(shape), dtype, offset=off)
        return Buf("sb", off, off + n, h.ap(), esz)

    def mark(self):
        return self.sb_ptr

    def reset(self, m):
        self.sb_ptr = m

    def op(self, eng, fn, reads=(), writes=(), dma=False, semkey=None):
        if dma and semkey is None:
            self.nkey += 1
            semkey = f"k{self.nkey}"
        o = Op(eng, fn, list(reads), list(writes), dma, semkey)
        st_items = list(self.state.items())
        for b in o.reads:
            for key, st in st_items:
                if key[0] == b.space and key[1] < b.hi and b.lo < key[2]:
                    if st[0] is not None:
                        o.deps.add(st[0])
        for b in o.writes:
            for key, st in st_items:
                if key[0] == b.space and key[1] < b.hi and b.lo < key[2]:
                    if st[0] is not None:
                        o.deps.add(st[0])
                    for r in st[1]:
                        o.deps.add(r)
        o.deps.discard(o)
        for b in o.reads:
            key = (b.space, b.lo, b.hi)
            st = self.state.setdefault(key, [None, []])
            if not o.dma:
                st[1] = [r for r in st[1] if r.dma or r.eng != o.eng]
            st[1].append(o)
        for b in o.writes:
            key = (b.space, b.lo, b.hi)
            for k2 in list(self.state.keys()):
                if k2 != key and k2[0] == b.space and b.lo <= k2[1] and k2[2] <= b.hi:
                    del self.state[k2]
            self.state[key] = [o, []]
        self.ops.append(o)
        return o

    def emit(self):
        nc = self.nc
        for o in self.ops:
            for d in o.deps:
                d.milestone = True
        cnt = {e: 0 for e in self.ENGS}
        dcnt = {}
        for o in self.ops:
            if o.dma:
                dcnt[o.semkey] = dcnt.get(o.semkey, 0) + 1
                o.cnt = dcnt[o.semkey]
            elif o.milestone:
                cnt[o.eng] += 1
                o.idx = cnt[o.eng]
        sem_names = [("c_" + e) for e in self.ENGS if e != "sp"] + ["d_" + str(k) for k in dcnt]
        self.n_sems = len(sem_names)
        sems = {}
        with ExitStack() as es:
            for n in sem_names:
                sems[n] = es.enter_context(nc.semaphore(n))
            block = es.enter_context(nc.Block())
            per = {e: [o for o in self.ops if o.eng == e] for e in self.ENGS}
            final_waits = [("d_" + str(k), 16 * v) for k, v in dcnt.items()]

            def run(eng_name, eng):
                known = {}
                for o in per[eng_name]:
                    need = {}
                    for d in o.deps:
                        if d.dma:
                            s, v = "d_" + str(d.semkey), 16 * d.cnt
                        else:
                            if d.eng == "pe" and eng_name == "pe" and not o.dma:
                                continue
                            s, v = "c_" + d.eng, d.idx
                        if need.get(s, 0) < v:
                            need[s] = v
                    for s, v in need.items():
                        if known.get(s, 0) < v:
                            eng.wait_ge(sems[s], v)
                            known[s] = v
                    ins = o.fn(eng)
                    if o.dma:
                        ins.then_inc(sems["d_" + str(o.semkey)], 16)
                    elif o.milestone:
                        ins.then_inc(sems["c_" + eng_name], 1)
                if eng_name == "sp":
                    for s, v in final_waits:
                        if known.get(s, 0) < v:
                            eng.wait_ge(sems[s], v)

            @block.sync
            def _(e):
                run("sp", e)

            @block.tensor
            def _(e):
                run("pe", e)

            @block.scalar
            def _(e):
                run("act", e)

            @block.vector
            def _(e):
                run("dve", e)

            @block.gpsimd
            def _(e):
                run("pool", e)


def build_program():
    nc = bass.Bass("TRN2", target_bir_lowering=False)
    S = Sched(nc)

    def din(name, shape, dt=F32):
        return nc.dram_tensor(name, list(shape), dt, kind="ExternalInput").ap()

    def dout(name, shape):
        return nc.dram_tensor(name, list(shape), F32, kind="ExternalOutput").ap()

    d_xp = din("xp", [512, 1024])
    d_xs = din("xs", [1024, 1024])
    d_condT = din("condT", [128, 8, 2])
    d_ck = din("ck", [256, 256])
    d_cv = din("cv", [256, 256])
    d_wmod = [din("wmod0", [1024, 3072]), din("wmod1", [1024, 3072])]
    d_bmod = [din("bmod0", [3072]), din("bmod1", [3072])]
    d_win0 = din("win0", [1024, 2048])
    d_wout0 = din("wout0", [1024, 1024])
    d_win1 = din("win1", [1024, 2560])
    d_wout1 = din("wout1", [1024, 1024])
    d_nwT = din("nwT", [128, 2, 8])
    d_qkw = din("qkw", [128])
    d_sink = din("sink", [16])
    d_identb = din("identb", [128, 128], BF16)
    d_identf = din("identf", [2, 2])
    d_sel = din("sel", [2, 2, 128])
    d_dftP = din("dftP", [256, 2, 256], BF16)
    d_dftC = din("dftC", [256, 2, 256], BF16)
    d_dftS = din("dftS", [1024, 2, 512], BF16)
    d_rope = din("rope", [512, 2, 64])
    d_masks = din("masks", [128, 4, 512], BF16)
    o_yp = dout("yp", [512, 1024])
    o_ys = dout("ys", [256, 1024])
    o_nk = dout("nk", [512, 256])
    o_nv = dout("nv", [512, 256])

    PS = []
    for i in range(8):
        h = nc.alloc_psum_tensor(f"ps{i}", [128, 512], F32)
        PS.append(Buf("ps", i * 2048, (i + 1) * 2048, h.ap(), 4))
    ring = {"g": [2, 3, 4, 5, 6, 7], "gi": 0, "t": [0, 1], "ti": 0}

    def nb():
        b = PS[ring["g"][ring["gi"] % len(ring["g"])]]
        ring["gi"] += 1
        return b

    def ntb():
        b = PS[ring["t"][ring["ti"] % 2]]
        ring["ti"] += 1
        return b

    XP = [S.sb(f"xp{t}", [128, 1024], F32) for t in range(4)]
    XL = [S.sb(f"xl{t}", [128, 1024], F32) for t in range(4)]
    wAu = [S.sb(f"wAu{i}", [128, 8, 512], BF16) for i in range(2)]
    wAz = [S.sb(f"wAz{i}", [128, 8, 512], BF16) for i in range(2)]
    wCq = [S.sb(f"wCq{i}", [128, 8, 512], BF16) for i in range(2)]
    wCkv = S.sb("wCkv", [128, 8, 512], BF16)
    wm = [S.sb(f"wm{i}", [128, 8, 256], BF16) for i in range(2)]
    bm = [S.sb(f"bm{i}", [2, 256], F32) for i in range(2)]
    mrow = [S.sb(f"mrow{i}", [2, 256], F32) for i in range(2)]
    gate = S.sb("gate", [128, 2, 1024], F32)
    MT = [S.sb(f"mt{l}", [128, 2, 8, 2], F32) for l in range(2)]
    identb = S.sb("identb", [128, 128], BF16)
    identf = S.sb("identf", [2, 2], F32)
    sel = S.sb("sel", [2, 2, 128], F32)
    nwT = S.sb("nwT", [128, 2, 8], F32)
    condT = S.sb("condT", [128, 8, 2], F32)
    scT = S.sb("scT", [128, 8, 2], BF16)
    ssb = S.sb("ssb", [128, 160], F32)
    rsb = S.sb("rsb", [128, 160], F32)
    mhalf = S.sb("mhalf", [128, 16], F32)
    ones3 = S.sb("ones3", [128, 192], BF16)
    qkw = S.sb("qkw", [128, 2, 64], F32)
    esink = S.sb("esink", [128, 16], F32)
    esinkP = S.sb("esinkP", [128, 4, 2], F32)
    off_wB = S.mark()
    wB = S.sb("wB", [128, 8, 1024], BF16)
    off_dftS = S.mark()
    dftS = S.sb("dftS", [128, 8, 2, 512], BF16)
    xn = [S.sb(f"xn{i}", [128, 1024], BF16) for i in range(2)]
    tmp = [S.sb(f"tmp{i}", [128, 512], F32) for i in range(2)]
    off_L = S.mark()
    dftP = S.sb("dftP", [128, 2, 2, 256], BF16)
    dftC = S.sb("dftC", [128, 2, 2, 256], BF16)
    hT = S.sb("hT", [128, 8, 1024], BF16)
    off_U = S.mark()
    U = S.sb("U", [128, 8, 512], BF16)
    AT = S.sb("AT", [128, 4, 2, 512], BF16)
    zs = [S.sb(f"zs{i}", [128, 512], BF16) for i in range(2)]
    yT = S.sb("yT", [128, 8, 512], BF16)
    xin = [S.sb(f"xin{i}", [128, 1024], F32) for i in range(2)]
    end_L0 = S.mark()
    gate1 = S.sb("gate1", [128, 2, 1024], F32, at=xin[0].lo)
    assert xin[1].hi == xin[0].lo + 8192
    gates = [gate, gate1]
    wm0 = wm + [S.sb("wmx0", [128, 8, 256], BF16, at=off_U), S.sb("wmx1", [128, 8, 256], BF16, at=off_U + 4096)]

    cnt = {"ss": 0, "ev": 0, "xn": 0, "tmp": 0, "wm": 0, "zs": 0, "bm": 0, "xin": 0}
    vcnt = {"n": 0}

    def vbuf(apv):
        vcnt["n"] += 1
        return Buf("v", vcnt["n"] * 10, vcnt["n"] * 10 + 1, apv, 2)

    hT_A, hT_B, hT_C = vbuf(hT.ap()), vbuf(hT.ap()), vbuf(hT.ap())

    def hT_of(col0, ncols=128):
        out = []
        if col0 < 256:
            out.append(hT_A)
        if col0 < 512 and col0 + ncols > 256:
            out.append(hT_B)
        if col0 + ncols > 512:
            out.append(hT_C)
        return out

    def dma(q, out_ap, in_ap, reads=(), writes=(), key=None):
        S.op(q, lambda e: e.dma_start(out=out_ap, in_=in_ap), reads=reads, writes=writes, dma=True, semkey=key)

    def mm(bank, out_ap, lhsT, rhs, start, stop, reads, sgc=False):
        if sgc:
            S.op("pe", lambda e: e.matmul(out_ap, lhsT=lhsT, rhs=rhs, start=start, stop=stop, skip_group_check=True),
                 reads=reads, writes=[bank])
        else:
            S.op("pe", lambda e: e.matmul(out_ap, lhsT=lhsT, rhs=rhs, start=start, stop=stop),
                 reads=reads, writes=[bank])

    def tr(bank, out_ap, in_ap, ident_ap, reads):
        S.op("pe", lambda e: e.transpose(out=out_ap, in_=in_ap, identity=ident_ap), reads=reads, writes=[bank])

    def evac(dst_ap, src_ap, reads, writes, eng=None):
        if eng is None:
            eng = "act" if cnt["ev"] % 2 == 0 else "dve"
            cnt["ev"] += 1
        if eng == "act":
            S.op("act", lambda e: e.activation(out=dst_ap, in_=src_ap, func=AF.Identity), reads=reads, writes=writes)
        else:
            S.op(eng, lambda e: e.tensor_copy(out=dst_ap, in_=src_ap), reads=reads, writes=writes)

    def next_ss(n=1):
        k = cnt["ss"]
        cnt["ss"] += n
        assert cnt["ss"] <= 160
        return ssb.col(k, k + n), rsb.col(k, k + n)

    def rstd_ops(ssc, rsc, n, inv):
        S.op("dve", lambda e: e.tensor_scalar(out=rsc.ap(), in0=ssc.ap(), scalar1=inv, scalar2=1e-6,
                                              op0=ALU.mult, op1=ALU.add), reads=[ssc], writes=[rsc])
        S.op("pool", lambda e: e.tensor_tensor(out=rsc.ap(), in0=rsc.ap(), in1=mhalf[:, 0:n], op=ALU.pow),
             reads=[rsc, mhalf], writes=[rsc])

    dma("sp", identb.ap(), d_identb, writes=[identb])
    dma("sp", identf.ap(), d_identf, writes=[identf])
    dma("sp", sel.ap(), d_sel, writes=[sel])
    dma("sp", nwT.ap(), d_nwT, writes=[nwT])
    dma("sp", condT.ap(), d_condT, writes=[condT])
    dma("sp", qkw.ap(), d_qkw.partition_broadcast(128).rearrange("p (a d) -> p a d", a=2), writes=[qkw])
    dma("sp", esink.ap(), d_sink.partition_broadcast(128), writes=[esink])
    S.op("pool", lambda e: e.memset(ssb.ap(), 0.0), writes=[ssb])
    S.op("pool", lambda e: e.memset(mhalf.ap(), -0.5), writes=[mhalf])
    S.op("pool", lambda e: e.memset(mhalf[:, 12:13], 1e-6), writes=[mhalf])
    S.op("pool", lambda e: e.memset(ones3.ap(), 1.0), writes=[ones3])
    S.op("pool", lambda e: e.memset(ones3[:, 64:128], 0.0), writes=[ones3])
    S.op("act", lambda e: e.activation(out=esink.ap(), in_=esink.ap(), func=AF.Exp), reads=[esink], writes=[esink])
    esv = esink.ap().rearrange("p (g a two) -> p g a two", g=4, a=2)
    S.op("dve", lambda e: e.tensor_copy(out=esinkP[0:64, :, :], in_=esv[0:64, :, :, 0]), reads=[esink], writes=[esinkP])
    S.op("dve", lambda e: e.tensor_copy(out=esinkP[64:128, :, :], in_=esv[64:128, :, :, 1]), reads=[esink],
         writes=[esinkP])
    S.op("act", lambda e: e.activation(out=scT.ap(), in_=condT.ap(), func=AF.Silu), reads=[condT], writes=[scT])
    for t in range(4):
        dma("sp", XP[t].ap(), d_xp[t * 128:(t + 1) * 128, :], writes=[XP[t]])

    def deferred_loads():
        for t in range(4):
            dma("sp", XL[t].ap(), d_xs[t * 128:(t + 1) * 128, :], reads=[wAz[1]], writes=[XL[t]])
        dma("sp", dftS.ap(), d_dftS.rearrange("(t p) f n -> p t f n", p=128), reads=[wAz[1]], writes=[dftS])

    mod_state = {}

    def mod_load(l, j, bufs):
        i = cnt["wm"] % len(bufs)
        cnt["wm"] += 1
        ib = cnt["bm"] % 2
        cnt["bm"] += 1
        dma("pool", bufs[i].ap(), d_wmod[l][:, j * 256:(j + 1) * 256].rearrange("(kc p) n -> p kc n", p=128),
            writes=[bufs[i]], key=f"wm{i}")
        mod_state[(l, j)] = (bufs[i], ib)

    def mod_a(l, j):
        wbuf, ib = mod_state[(l, j)]
        dma("sp", bm[ib].ap(), d_bmod[l][j * 256:(j + 1) * 256].partition_broadcast(2), writes=[bm[ib]],
            key=f"bm{ib}")
        bank = nb()
        for kc in range(8):
            mm(bank, bank[0:2, 0:256], scT[:, kc, :], wbuf[:, kc, :], kc == 0, kc == 7, [scT, wbuf])
        S.op("dve", lambda e: e.tensor_tensor(out=mrow[ib].ap(), in0=bank[0:2, 0:256], in1=bm[ib].ap(), op=ALU.add),
             reads=[bank, bm[ib]], writes=[mrow[ib]])

    def mod_b(l, j):
        wbuf, ib = mod_state[(l, j)]
        which = j // 4
        if which < 2:
            b2 = nb()
            for h in range(2):
                tr(b2, b2[:, h * 2:(h + 1) * 2], mrow[ib][:, h * 128:(h + 1) * 128], identf.ap(), [mrow[ib], identf])
            kc0 = (j % 4) * 2
            src = b2[:, 0:4].rearrange("p (h r) -> p h r", h=2)
            if which == 0:
                S.op("dve", lambda e: e.tensor_copy(out=MT[l][:, 0, kc0:kc0 + 2, :], in_=src),
                     reads=[b2], writes=[MT[l]])
            else:
                nwb = nwT[:, l, kc0:kc0 + 2].unsqueeze(2).to_broadcast([128, 2, 2])
                S.op("dve", lambda e: e.scalar_tensor_tensor(out=MT[l][:, 1, kc0:kc0 + 2, :], in0=src, scalar=1.0,
                                                             in1=nwb, op0=ALU.add, op1=ALU.mult),
                     reads=[b2, nwT], writes=[MT[l]])
        else:
            for r in range(2):
                b3 = nb()
                mm(b3, b3[:, 0:256], sel[:, r, :], mrow[ib].ap(), True, True, [sel, mrow[ib]])
                c0 = (j - 8) * 256
                evac(gates[l][:, r, c0:c0 + 256], b3[:, 0:256], [b3], [gates[l]], eng="act")

    def prep_A(xbuf, on_dve=False):
        ssc, rsc = next_ss()
        xi = xn[cnt["xn"] % 2]
        cnt["xn"] += 1
        if on_dve:
            S.op("dve", lambda e: e.scalar_tensor_tensor(out=xi.ap(), in0=xbuf.ap(), scalar=1.0, in1=xbuf.ap(),
                                                         op0=ALU.mult, op1=ALU.mult, accum_out=ssc.ap()),
                 reads=[xbuf], writes=[xi, ssc])
            rstd_ops(ssc, rsc, 1, 1.0 / 1024)
            S.op("dve", lambda e: e.tensor_scalar(out=xi.ap(), in0=xbuf.ap(), scalar1=rsc.ap(), scalar2=None,
                                                  op0=ALU.mult), reads=[xbuf, rsc], writes=[xi])
            return xi
        S.op("act", lambda e: e.activation(out=xi.ap(), in_=xbuf.ap(), func=AF.Square, accum_out=ssc.ap()),
             reads=[xbuf], writes=[xi, ssc])
        rstd_ops(ssc, rsc, 1, 1.0 / 1024)
        S.op("act", lambda e: e.activation(out=xi.ap(), in_=xbuf.ap(), func=AF.Identity, scale=rsc.ap()),
             reads=[xbuf, rsc], writes=[xi])
        return xi

    def prep_B(l, xi, r, hbuf, col0, hdeps=None, shift_eng="dve"):
        hdeps = hdeps if hdeps is not None else [hbuf]
        tb = ntb()
        tbv = tb.ap().bitcast(BF16)
        for kc in range(8):
            tr(tb, tbv[:, kc * 128:(kc + 1) * 128], xi[:, kc * 128:(kc + 1) * 128], identb.ap(), [xi, identb])
        hv = hbuf[:, :, col0:col0 + 128]
        scb = MT[l][:, 1, :, r:r + 1].to_broadcast([128, 8, 128])
        shb = MT[l][:, 0, :, r:r + 1].to_broadcast([128, 8, 128])
        S.op("dve", lambda e: e.tensor_tensor(out=hv, in0=tbv.rearrange("p (k q) -> p k q", k=8), in1=scb, op=ALU.mult),
             reads=[tb, MT[l]], writes=hdeps)
        S.op(shift_eng, lambda e: e.tensor_tensor(out=hv, in0=hv, in1=shb, op=ALU.add), reads=hdeps + [MT[l]],
             writes=hdeps)

    def prep_seq(l, tiles, r, hbuf, rest_src=None, pre=None, cols=None, depf=None, tidx=None):
        nt = len(tiles)
        xis = list(pre) if pre else []
        for step in range(nt + 1):
            if step < nt and step >= len(xis):
                xb = tiles[step]
                if xb is None:
                    xb = xin[cnt["xin"] % 2]
                    dma("sp", xb.ap(), rest_src(tidx[step]), writes=[xb], key=f"xin{cnt['xin'] % 2}")
                    cnt["xin"] += 1
                xis.append(prep_A(xb))
            if step >= 1:
                c0 = cols[step - 1] if cols else (step - 1) * 128
                prep_B(l, xis[step - 1], r, hbuf, c0, depf(c0) if depf else None)

    def prep_stages(l, tiles, r, hbuf, cols, depf, rest_src=None, tidx=None, dve_prep=False):
        st = []
        xis = {}
        nt = len(tiles)

        def mk(step):
            def f():
                if step >= 2:
                    prep_B(l, xis[step - 2], r, hbuf, cols[step - 2], depf(cols[step - 2]),
                           shift_eng=("pool" if (l == 1 and dve_prep) else "dve"))
                if step < nt:
                    xb = tiles[step]
                    if xb is None:
                        xb = xin[cnt["xin"] % 2]
                        dma("sp", xb.ap(), rest_src(tidx[step]), writes=[xb], key=f"xin{cnt['xin'] % 2}")
                        cnt["xin"] += 1
                    xis[step] = prep_A(xb, on_dve=(l == 1 and dve_prep and step % 2 == 0))
            return f

        for step in range(nt + 2):
            st.append(mk(step))
        return st

    def residual_update(bank, xbuf, r, n, l=0):
        tp = tmp[cnt["tmp"] % 2]
        cnt["tmp"] += 1
        gl = gates[l]
        S.op("dve", lambda e: e.tensor_tensor(out=tp.ap(), in0=bank[:, 0:512], in1=gl[:, r, n * 512:(n + 1) * 512],
                                              op=ALU.mult), reads=[bank, gl], writes=[tp])
        S.op("pool", lambda e: e.tensor_tensor(out=xbuf[:, n * 512:(n + 1) * 512], in0=xbuf[:, n * 512:(n + 1) * 512],
                                               in1=tp.ap(), op=ALU.add), reads=[xbuf, tp], writes=[xbuf])

    def l0_main(nt, n_local, dft_ap, dft_buf, hooks=None, cb=0, stage_q=None):
        ncols = n_local * 128
        stage_q = list(stage_q or [])
        n_stage = len(stage_q)
        n_points = 2 * (nt + (4 if ncols == 256 else 8) + 4)
        pt = {"i": 0, "done": 0}

        def point():
            pt["i"] += 1
            target = (pt["i"] * n_stage) // n_points
            while pt["done"] < target and stage_q:
                stage_q.pop(0)()
                pt["done"] += 1
        for gp in range(2):
            for t in range(nt):
                bank = nb()
                for kc in range(8):
                    mm(bank, bank[:, 0:512], hT[:, kc, cb + t * 128:cb + (t + 1) * 128],
                       wAu[gp][:, kc, :], kc == 0, kc == 7, hT_of(cb + t * 128) + [wAu[gp]])
                evac(U[:, t, :], bank[:, 0:512], [bank], [U])
                point()
            for cc in range(4):
                if ncols == 256:
                    bank = nb()
                    for f in range(2):
                        for t in range(nt):
                            mm(bank, bank[:, f * 256:(f + 1) * 256], U[:, t, cc * 128:(cc + 1) * 128], dft_ap(t, f),
                               f == 0 and t == 0, t == nt - 1, [U, dft_buf], sgc=True)
                    evac(AT[:, cc, :, 0:256], bank[:, 0:512].rearrange("p (f q) -> p f q", f=2), [bank], [AT])
                    point()
                    continue
                for f in range(2):
                    bank = nb()
                    for t in range(nt):
                        mm(bank, bank[:, 0:ncols], U[:, t, cc * 128:(cc + 1) * 128], dft_ap(t, f),
                           t == 0, t == nt - 1, [U, dft_buf])
                    evac(AT[:, cc, f, 0:ncols], bank[:, 0:ncols], [bank], [AT])
                    point()
            if hooks and ("afterA", gp) in hooks:
                hooks[("afterA", gp)]()
            for e2 in range(4):
                e_ = gp * 4 + e2
                g, cp = e_ // 2, e_ % 2
                zb = nb()
                for kc in range(8):
                    mm(zb, zb[:, 0:ncols], wAz[gp][:, kc, e2 * 128:(e2 + 1) * 128], hT[:, kc, cb:cb + ncols],
                       kc == 0, kc == 7, [wAz[gp]] + hT_of(cb, ncols))
                zi = zs[cnt["zs"] % 2]
                cnt["zs"] += 1
                S.op("act", lambda e, zb=zb, zi=zi: e.activation(out=zi[:, 0:ncols], in_=zb[:, 0:ncols], func=AF.Silu),
                     reads=[zb], writes=[zi])
                yb = nb()
                k = 0
                for cch in range(2):
                    for f in range(2):
                        mm(yb, yb[:, 0:ncols], dftC[:, cch, f, cp * 128:(cp + 1) * 128],
                           AT[:, (g - gp * 2) * 2 + cch, f, 0:ncols], k == 0, k == 3, [dftC, AT])
                        k += 1
                S.op("dve", lambda e, yb=yb, zi=zi, e_=e_: e.tensor_tensor(out=yT[:, e_, 0:ncols], in0=yb[:, 0:ncols],
                                                                           in1=zi[:, 0:ncols], op=ALU.mult),
                     reads=[yb, zi], writes=[yT])
                if hooks and ("afterE", e_) in hooks:
                    hooks[("afterE", e_)]()
                point()
        while stage_q:
            stage_q.pop(0)()
        if hooks and "afterZ" in hooks:
            hooks["afterZ"]()

    def l0_out(tiles, r, n_local, side=()):
        side = list(side)
        n_side, n_grp, gi_, done_ = len(side), 2 * n_local, 0, 0
        for t in range(n_local):
            for n in range(2):
                gi_ += 1
                while side and done_ < -(-(gi_ * n_side) // n_grp):
                    side.pop(0)()
                    done_ += 1
                ob = nb()
                for kc in range(8):
                    mm(ob, ob[:, 0:512], yT[:, kc, t * 128:(t + 1) * 128], wB[:, kc, n * 512:(n + 1) * 512],
                       kc == 0, kc == 7, [yT, wB])
                residual_update(ob, tiles[t], r, n)
        for f_ in side:
            f_()

    for j in range(4):
        mod_load(0, j, wm0)
    pre0 = [prep_A(XP[0]), prep_A(XP[1])]
    for j in range(8):
        mod_a(0, j)
        mod_b(0, j)
        if j + 4 < 10:
            mod_load(0, j + 4, wm0)
        if j == 3:
            dma("pool", wAu[0].ap(), d_win0[:, 0:512].rearrange("(kc p) n -> p kc n", p=128), writes=[wAu[0]])
        if j == 2:
            dma("sp", dftP.ap(), d_dftP.rearrange("(t p) f n -> p t f n", p=128), writes=[dftP])
            dma("sp", dftC.ap(), d_dftC.rearrange("(t p) f n -> p t f n", p=128), writes=[dftC])
    dma("pool", wAz[0].ap(), d_win0[:, 1024:1536].rearrange("(kc p) n -> p kc n", p=128), reads=[MT[0]], writes=[wAz[0]])
    dma("pool", wAu[1].ap(), d_win0[:, 512:1024].rearrange("(kc p) n -> p kc n", p=128), reads=[MT[0]], writes=[wAu[1]])

    def late_weights(which):
        if which == 0:
            dma("pool", wAz[1].ap(), d_win0[:, 1536:2048].rearrange("(kc p) n -> p kc n", p=128), reads=[wAz[0]],
                writes=[wAz[1]])
        else:
            dma("pool", wB.ap(), d_wout0.rearrange("(kc p) n -> p kc n", p=128), reads=[wAu[1]], writes=[wB])

    rest_src = lambda t: d_xs[t * 128:(t + 1) * 128, :]
    prep_seq(0, [XP[0], XP[1]], 0, hT, pre=pre0, cols=[0, 128], depf=hT_of)
    hk0 = {}
    st1_ = prep_stages(0, [XP[2], XP[3]], 0, hT, [256, 384], hT_of)

    def mk_a(l, j, nxt):
        def hk():
            mod_a(l, j)
            if nxt is not None:
                mod_load(l, nxt, wm)
        return hk

    def mk_b(l, j):
        return lambda: mod_b(l, j)

    q0_ = []
    for j in range(8, 12):
        q0_ += [mk_a(0, j, j + 2 if j + 2 < 12 else None), mk_b(0, j)]
    def mk_both(e_):
        def hk():
            q0_[e_]()
            if e_ == 1:
                late_weights(0)
            if e_ == 2:
                late_weights(1)
            if e_ in (1, 3, 5, 7):
                st1_[(e_ - 1) // 2]()
        return hk

    for e_ in range(8):
        hk0[("afterE", e_)] = mk_both(e_)
    l0_main(2, 2, lambda t, f: dftP[:, t, f, :], dftP, cb=0, hooks=hk0)
    cnt["wm"] = 0
    q1_ = []
    for j in range(12):
        q1_ += [mk_a(1, j, j + 2 if j + 2 < 12 else None), mk_b(1, j)]

    def mk_multi(fs):
        return lambda: [f() for f in fs]

    l0_out([XP[0], XP[1]], 0, 2,
           side=prep_stages(0, [None] * 4, 1, hT, [512, 640, 768, 896], hT_of, rest_src=rest_src, tidx=[4, 5, 6, 7]))
    deferred_loads()
    l0_main(2, 2, lambda t, f: dftP[:, t, f, :], dftP, cb=256)
    mod_load(1, 0, wm)
    mod_load(1, 1, wm)
    l0_out([XP[2], XP[3]], 0, 2, side=prep_stages(0, XL, 1, hT, [0, 128, 256, 384], hT_of))

    hooks = {}
    sq_ = list(q1_)
    sq_.insert(4, lambda: dma("pool", wCkv.ap(), d_win1[:, 1024:1536].rearrange("(kc p) n -> p kc n", p=128),
                              writes=[wCkv]))
    sq_.insert(11, lambda: dma("pool", wCq[0].ap(), d_win1[:, 0:512].rearrange("(kc p) n -> p kc n", p=128),
                               writes=[wCq[0]]))
    sq_.insert(18, lambda: dma("pool", wCq[1].ap(), d_win1[:, 512:1024].rearrange("(kc p) n -> p kc n", p=128),
                               writes=[wCq[1]]))
    wCz_holder = {}

    def load_wCz():
        wCz_holder["b"] = S.sb("wCz", [128, 8, 1024], BF16, at=off_dftS)
        dma("pool", wCz_holder["b"].ap(), d_win1[:, 1536:2560].rearrange("(kc p) n -> p kc n", p=128),
            writes=[wCz_holder["b"]])

    hooks[("afterA", 1)] = load_wCz
    ckb = S.sb("ckb", [128, 2, 256], BF16, at=wAz[0].lo + 12288)
    cvb = S.sb("cvb", [128, 2, 256], BF16, at=wAz[0].lo + 13312)

    def after_z():
        dma("pool", ckb.ap(), d_ck.rearrange("(t p) n -> p t n", p=128), writes=[ckb])
        dma("pool", cvb.ap(), d_cv.rearrange("(t p) n -> p t n", p=128), writes=[cvb])
        for n_ in range(2):
            dma("pool", wAu[n_].ap(), d_wout1[:, n_ * 512:(n_ + 1) * 512].rearrange("(kc p) n -> p kc n", p=128),
                writes=[wAu[n_]])

    hooks["afterZ"] = after_z
    l0_main(8, 4, lambda t, f: dftS[:, t, f, :], dftS, hooks=hooks, stage_q=sq_)
    wCz = wCz_holder["b"]

    S.reset(off_wB)
    QT = [S.sb(f"QT{i}", [64, 16, 128], BF16) for i in range(2)]
    KT = S.sb("KT", [64, 4, 512], BF16)
    VVc = S.sb("VVc", [128, 2, 4, 192], BF16)
    assert S.mark() <= off_dftS
    S.reset(wAz[0].lo)
    masks = S.sb("masks", [128, 4, 512], BF16)
    qf2 = S.sb("qf2", [128, 16, 64], F32)
    rb2 = S.sb("rb2", [128, 16, 64], F32)
    assert S.mark() <= wAz[1].hi
    S.reset(off_L)
    h1T = S.sb("h1T", [128, 8, 512], BF16)
    assert S.mark() <= off_U
    h1A, h1B = vbuf(h1T.ap()), vbuf(h1T.ap())

    def h1_of(col0, ncols=128):
        out = []
        if col0 < 256:
            out.append(h1A)
        if col0 + ncols > 256:
            out.append(h1B)
        return out
    qf = S.sb("qf", [128, 16, 64], F32)
    rb = S.sb("rb", [128, 16, 64], F32)
    qn = [S.sb(f"qn{i}", [128, 1024], BF16) for i in range(2)]
    kf = [S.sb(f"kf{i}", [128, 4, 64], F32) for i in range(2)]
    ksq = S.sb("ksq", [128, 256], F32)
    krb = S.sb("krb", [128, 4, 64], F32)
    ksq2 = S.sb("ksq2", [128, 256], F32, at=wAz[0].lo + 14336)
    krb2 = S.sb("krb2", [128, 4, 64], F32, at=wAz[0].lo + 15360)
    kn = [S.sb(f"kn{i}", [128, 256], BF16) for i in range(4)]
    vf = [S.sb(f"vf{i}", [128, 256], F32) for i in range(2)]
    KTc = S.sb("KTc", [64, 4, 256], BF16)
    VV = S.sb("VV", [128, 4, 4, 192], BF16)
    PT = [S.sb(f"PT{i}", [128, 512], BF16) for i in range(3)]
    zs1T = S.sb("zs1T", [128, 8, 256], BF16, at=gate.lo)
    off_gT = gate.lo + 4096
    gT = S.sb("gT", [128, 8, 256], BF16, at=off_gT)

    Lp = [S.sb(f"Lp{i}", [128, 2, 128], F32) for i in range(3)]
    ropeT = S.sb("ropeT", [128, 4, 2, 64], F32)
    ropeW = [S.sb("ropeW0", [128, 4, 2, 64], F32), ropeT]
    assert S.mark() <= xin[0].lo, (S.mark(), xin[0].lo)
    c1 = {"qt": 0, "kf": 0, "vf": 0, "pt": 0, "ol": 0, "kn": 0, "lp": 0}

    S.op("pool", lambda e: e.memset(mhalf[:, 8:9], -0.5), reads=[hT_A, hT_B, hT_C],
         writes=[hT, dftP, dftC, h1A, h1B])
    ps_ = prep_stages(1, [XP[0], XP[1]], 0, h1T, [0, 128], h1_of)
    noop_ = lambda: None
    ps1_ = prep_stages(1, [XP[2], XP[3]], 0, h1T, [256, 384], h1_of)
    l0_out(XL, 1, 4, side=[ps_[0], ps_[1], ps_[2], ps_[3], ps1_[0], ps1_[1], ps1_[2], ps1_[3]])

    ring["g"] = [2, 3, 4]
    dma("sp", ropeT.ap(), d_rope.rearrange("(t p) a d -> p t a d", p=128), writes=[ropeT])
    dma("sp", masks.ap(), d_masks, writes=[masks])
    S.op("pool", lambda e: e.memset(VV[:, :, :, 64:128], 0.0), writes=[VV])

    def rope_tables():
      S.op("pool", lambda e: e.memset(VVc[:, :, :, 64:128], 0.0), writes=[VVc])
      for wi in range(2):
          rw = ropeW[wi]
          S.op("pool", lambda e, rw=rw, wi=wi: e.tensor_tensor(
              out=rw[:, :, 0, :], in0=ropeT[:, :, 0, :], in1=qkw[:, wi:wi + 1, :].to_broadcast([128, 4, 64]), op=ALU.mult),
               reads=[ropeT, qkw], writes=[rw])
          S.op("pool", lambda e, rw=rw, wi=wi: e.tensor_tensor(
              out=rw[:, :, 1, 0:32], in0=ropeT[:, :, 1, 0:32], in1=qkw[:, wi:wi + 1, 32:64].to_broadcast([128, 4, 32]),
              op=ALU.mult), reads=[ropeT, qkw], writes=[rw])
          S.op("pool", lambda e, rw=rw, wi=wi: e.tensor_tensor(
              out=rw[:, :, 1, 32:64], in0=ropeT[:, :, 1, 32:64], in1=qkw[:, wi:wi + 1, 0:32].to_broadcast([128, 4, 32]),
              op=ALU.mult), reads=[ropeT, qkw], writes=[rw])

    def head_norm(src_banks, nh, dst, sq=None):
        ssc, rsc = next_ss(nh)
        if sq is None:
            sq = rb if nh == 16 else ksq
        h0 = 0
        for (bank, c0, n) in src_banks:
            sqv = (sq.ap().rearrange("p a d -> p (a d)") if nh == 16 else sq.ap())[:, h0 * 64:(h0 + n) * 64]
            S.op("act", lambda e, bank=bank, c0=c0, n=n, sqv=sqv: e.activation(out=sqv, in_=bank[:, c0:c0 + n * 64],
                                                                               func=AF.Square),
                 reads=[bank], writes=[sq])
            h0 += n
        sq3 = sq.ap() if nh == 16 else sq.ap().rearrange("p (a d) -> p a d", d=64)
        S.op("dve", lambda e: e.tensor_reduce(out=ssc.ap(), in_=sq3, axis=AX.X, op=ALU.add), reads=[sq], writes=[ssc])
        S.op("act", lambda e: e.activation(out=rsc.ap(), in_=ssc.ap(), func=AF.Ln, scale=1.0 / 64, bias=mhalf[:, 12:13]),
             reads=[ssc, mhalf], writes=[rsc])
        S.op("act", lambda e: e.activation(out=rsc.ap(), in_=rsc.ap(), func=AF.Exp, scale=-0.5), reads=[rsc], writes=[rsc])
        h0 = 0
        for (bank, c0, n) in src_banks:
            S.op("dve", lambda e, bank=bank, c0=c0, n=n, h0=h0: e.tensor_tensor(
                out=dst[:, h0:h0 + n, :], in0=bank[:, c0:c0 + n * 64].rearrange("p (a d) -> p a d", d=64),
                in1=rsc[:, h0:h0 + n].unsqueeze(2).to_broadcast([128, n, 64]), op=ALU.mult),
                 reads=[bank, rsc], writes=[dst])
            h0 += n

    def rope_ops(x, tmpb, nh, tile, out_ap, out_buf, wi, eng="dve", hr=None):
        rw = ropeW[wi]
        if hr is None:
            xs, ts, os_, xa, ta, oa, n = x, tmpb, out_buf, x.ap(), tmpb.ap(), out_ap, nh
        else:
            h0, h1 = hr
            n = h1 - h0
            xs = Buf(x.space, x.lo + h0 * 256, x.lo + h1 * 256, x.apv, x.esz)
            ts = Buf(tmpb.space, tmpb.lo + h0 * 256, tmpb.lo + h1 * 256, tmpb.apv, tmpb.esz)
            os_ = Buf(out_buf.space, out_buf.lo + h0 * 128, out_buf.lo + h1 * 128, out_buf.apv, out_buf.esz)
            xa, ta, oa = x[:, h0:h1, :], tmpb[:, h0:h1, :], out_ap[:, h0:h1, :]
        CC = rw[:, tile, 0:1, :].to_broadcast([128, n, 64])
        S1 = rw[:, tile, 1:2, 0:32].to_broadcast([128, n, 32])
        S2 = rw[:, tile, 1:2, 32:64].to_broadcast([128, n, 32])
        S.op(eng, lambda e: e.tensor_tensor(out=ta[:, :, 0:32], in0=xa[:, :, 32:64], in1=S1, op=ALU.mult),
             reads=[xs, rw], writes=[ts])
        S.op(eng, lambda e: e.tensor_tensor(out=ta[:, :, 32:64], in0=xa[:, :, 0:32], in1=S2, op=ALU.mult),
             reads=[xs, rw], writes=[ts])
        S.op(eng, lambda e: e.tensor_tensor(out=xa, in0=xa, in1=CC, op=ALU.mult),
             reads=[xs, rw], writes=[xs])
        S.op(eng, lambda e: e.tensor_tensor(out=oa, in0=xa, in1=ta, op=ALU.add),
             reads=[xs, ts], writes=[os_])

    def vv_fill(dst, t, src_ap, reads):
        dv = dst[:, t, :, :].rearrange("p g (three d) -> p g three d", three=3)
        S.op("dve", lambda e: e.tensor_copy(out=dv[:, :, 0, :], in_=src_ap), reads=reads, writes=[dst])
        S.op("act", lambda e: e.activation(out=dv[:, :, 2, :], in_=src_ap, func=AF.Identity), reads=reads, writes=[dst])

    def l1_front(tiles, r, qtiles, is_sample, o_k, o_v, kv_row0, mid_hook=None, cb=0, late_hook=None):
        nt = len(tiles)
        tr_eng = None if is_sample else "dve"
        ring["g"] = [2, 3, 4, 5, 6, 7]
        ring["gi"] = 0
        kns = {}
        qts = {}

        def k_part(t):
            kvb = nb()
            for kc in range(8):
                mm(kvb, kvb[:, 0:512], h1T[:, kc, cb + t * 128:cb + (t + 1) * 128], wCkv[:, kc, :], kc == 0, kc == 7,
                   h1_of(cb + t * 128) + [wCkv])
            kfi = kf[c1["kf"] % 2]
            c1["kf"] += 1
            kni = kn[c1["kn"] % 4]
            c1["kn"] += 1
            kns[t] = kni
            head_norm([(kvb, 0, 4)], 4, kfi, sq=(ksq if t % 2 == 0 else ksq2))
            knv = kni.ap().rearrange("p (a d) -> p a d", d=64)
            if is_sample:
                rope_ops(kfi, krb if t % 2 == 0 else krb2, 4, t, knv, kni, 1, eng=("dve" if t % 2 == 0 else "pool"))
            else:
                S.op("dve", lambda e, kfi=kfi: e.tensor_tensor(out=kfi.ap(), in0=kfi.ap(),
                                                               in1=qkw[:, 1:2, :].to_broadcast([128, 4, 64]), op=ALU.mult),
                     reads=[kfi, qkw], writes=[kfi])
                S.op("act", lambda e, kfi=kfi, knv=knv: e.activation(out=knv, in_=kfi.ap(), func=AF.Identity),
                     reads=[kfi], writes=[kni])
                dma("sp", o_k[kv_row0 + t * 128:kv_row0 + (t + 1) * 128, :],
                    kfi.ap().rearrange("p a d -> p (a d)"), reads=[kfi], key=f"nk{c1['kf'] % 2}")
                vfi = vf[c1["vf"] % 2]
                c1["vf"] += 1
                S.op("act", lambda e, vfi=vfi, kvb=kvb: e.activation(out=vfi.ap(), in_=kvb[:, 256:512], func=AF.Identity),
                     reads=[kvb], writes=[vfi])
                dma("sp", o_v[kv_row0 + t * 128:kv_row0 + (t + 1) * 128, :], vfi.ap(), reads=[vfi],
                    key=f"nv{c1['vf'] % 2}")
            vv_fill(VV, t, kvb[:, 256:512].rearrange("p (g d) -> p g d", d=64), [kvb])

        def q_part(qi):
            t = qtiles[qi]
            qbanks = [nb(), nb()]
            for n in range(2):
                for kc in range(8):
                    mm(qbanks[n], qbanks[n][:, 0:512], h1T[:, kc, cb + t * 128:cb + (t + 1) * 128],
                       wCq[n][:, kc, :], kc == 0, kc == 7, h1_of(cb + t * 128) + [wCq[n]])
            qfi, rbi = (qf, rb) if qi == 0 else (qf2, rb2)
            head_norm([(qbanks[0], 0, 8), (qbanks[1], 0, 8)], 16, qfi, sq=rbi)
            qni = qn[c1["qt"] % 2]
            QTi = QT[c1["qt"] % 2]
            c1["qt"] += 1
            qnv = qni.ap().rearrange("p (a d) -> p a d", d=64)
            if is_sample:
                if qi == 0:
                    rope_ops(qfi, rbi, 16, t, qnv, qni, 0, eng="dve")
                else:
                    rope_ops(qfi, rbi, 16, t, qnv, qni, 0, eng="pool", hr=(10, 16))
                    rope_ops(qfi, rbi, 16, t, qnv, qni, 0, eng="dve", hr=(0, 10))
            else:
                S.op("dve", lambda e, qnv=qnv, qfi=qfi: e.tensor_tensor(
                    out=qnv, in0=qfi.ap(), in1=qkw[:, 0:1, :].to_broadcast([128, 16, 64]), op=ALU.mult),
                     reads=[qfi, qkw], writes=[qni])
            qts[qi] = (qni, QTi)

        def z_part(e_lo=0, e_hi=8):
            q0 = qtiles[0] * 128
            nq = len(qtiles) * 128
            for e_ in range(e_lo, e_hi):
                zb = nb()
                for kc in range(8):
                    mm(zb, zb[:, 0:nq], wCz[:, kc, e_ * 128:(e_ + 1) * 128], h1T[:, kc, cb + q0:cb + q0 + nq],
                       kc == 0, kc == 7, [wCz] + h1_of(cb + q0, nq))
                S.op("act", lambda e, zb=zb, e_=e_: e.activation(out=zs1T[:, e_, 0:nq], in_=zb[:, 0:nq], func=AF.Silu),
                     reads=[zb], writes=[zs1T])

        def k_tr(t):
            tb = ntb()
            tbv = tb.ap().bitcast(BF16)
            for g in range(4):
                tr(tb, tbv[0:64, g * 128:(g + 1) * 128], kns[t][:, g * 64:(g + 1) * 64], identb.ap(), [kns[t], identb])
            evac(KT[:, :, t * 128:(t + 1) * 128], tbv[0:64, 0:512].rearrange("p (g q) -> p g q", g=4), [tb], [KT],
                 eng=tr_eng)

        def q_tr(qi):
            qni, QTi = qts[qi]
            for half in range(2):
                tb = ntb()
                tbv = tb.ap().bitcast(BF16)
                for hh in range(8):
                    slot = half * 8 + hh
                    g, s_ = slot // 4, slot % 4
                    h = 4 * g + 2 * (s_ % 2) + (s_ // 2)
                    tr(tb, tbv[0:64, hh * 128:(hh + 1) * 128], qni[:, h * 64:(h + 1) * 64], identb.ap(), [qni, identb])
                evac(QTi[:, half * 8:(half + 1) * 8, :], tbv[0:64, 0:1024].rearrange("p (a q) -> p a q", a=8),
                     [tb], [QTi], eng=tr_eng)

        early_k = list(range(nt)) if not (is_sample and SPLIT_SAMPLE_FRONT) else [0, 1, 2]
        early_q = list(range(len(qtiles))) if not (is_sample and SPLIT_SAMPLE_FRONT) else [0]
        for t in early_k:
            k_part(t)
        for qi in early_q:
            q_part(qi)
        if mid_hook:
            mid_hook()
        z_part(0, 4)
        for t in early_k:
            k_tr(t)
        for qi in early_q:
            q_tr(qi)
        if late_hook:
            late_hook()
        z_part(4, 8)
        ring["g"] = [2, 3, 4]
        late = {}
        if is_sample and SPLIT_SAMPLE_FRONT:
            late = {1: (lambda: k_part(3)), 4: (lambda: q_part(1)), 12: (lambda: k_tr(3)), 15: (lambda: q_tr(1))}
            QT_second = QT[c1["qt"] % 2]
            return [qts[0][1], QT_second], late
        return [qts[qi][1] for qi in range(len(qtiles))], late

    def l1_attn(tiles, r, qtiles, is_sample, QTs, o_y, y_row0, side=(), side_at=None):
        side = list(side)
        side_at = dict(side_at or {})
        steps = []
        for qi in range(len(qtiles)):
            if is_sample:
                loc = [(0, 0), (1, None), (2, 2)] if qi == 0 else [(1, 1), (2, None), (3, 3)]
                chunks = [("l", kt, m) for kt, m in loc] + [("c", 0, None), ("c", 1, None)]
            else:
                chunks = [("l", 0, None), ("l", 1, None)]
            for g in range(4):
                for ci, ch in enumerate(chunks):
                    steps.append((qi, g, ci, len(chunks)) + ch)
        LA = 2
        n_fill = 1
        live = {}
        unit_bank = {}
        pending_out = []

        def emit_out(qi):
            t = qtiles[qi]
            for n in range(2):
                ob = nb()
                for kc in range(8):
                    mm(ob, ob[:, 0:512], gT[:, kc, qi * 128:(qi + 1) * 128], wAu[n][:, kc, :],
                       kc == 0, kc == 7, [gT, wAu[n]])
                residual_update(ob, tiles[t], r, n, l=1)
            dma("sp", o_y[y_row0 + qi * 128:y_row0 + (qi + 1) * 128, :], tiles[t].ap(), reads=[tiles[t]])

        def emit_post(ob, g, qi):
            lp = Lp[c1["lp"] % 3]
            rz = lp
            c1["lp"] += 1
            S.op("dve", lambda e, lp=lp, ob=ob, g=g: e.tensor_tensor(
                out=lp.ap(), in0=ob[:, 256:512].rearrange("p (a q) -> p a q", a=2),
                in1=esinkP[:, g, :].unsqueeze(2).to_broadcast([128, 2, 128]), op=ALU.add),
                 reads=[ob, esinkP], writes=[lp])
            S.op("act", lambda e, lp=lp: e.activation(out=lp.ap(), in_=lp.ap(), func=AF.Ln),
                 reads=[lp], writes=[lp])
            S.op("act", lambda e, lp=lp: e.activation(out=lp.ap(), in_=lp.ap(), func=AF.Exp, scale=-1.0),
                 reads=[lp], writes=[lp])
            S.op("dve", lambda e, lp=lp, rz=rz, g=g, qi=qi: e.tensor_tensor(
                out=rz.ap(), in0=lp.ap(), in1=zs1T[:, 2 * g:2 * g + 2, qi * 128:(qi + 1) * 128], op=ALU.mult),
                 reads=[lp, zs1T], writes=[rz])
            S.op("dve", lambda e, rz=rz, ob=ob, g=g, qi=qi: e.tensor_tensor(
                out=gT[:, 2 * g:2 * g + 2, qi * 128:(qi + 1) * 128],
                in0=ob[:, 0:256].rearrange("p (a q) -> p a q", a=2), in1=rz.ap(), op=ALU.mult),
                 reads=[ob, rz], writes=[gT])
            if g == 3:
                pending_out.append([qi, 7 if is_sample else 12])

        post_q = []
        side_every = max(1, (len(steps) - 2) // max(1, len(side)))
        for k in range(len(steps) + LA):
            if side and k >= 1 and (k - 1) % side_every == 0:
                side.pop(0)()
            if k in side_at:
                side_at.pop(k)()
            if k < len(steps):
                qi, g, ci, nch, kind, kt, m = steps[k]
                QTi = QTs[qi]
                sbk = nb()
                kbuf = KT if kind == "l" else KTc
                for _ in range(n_fill):
                    mm(sbk, sbk[:, 0:256], identb.ap(), wCq[0][:, 0, 0:256], True, True, [identb, wCq[0]])
                mm(sbk, sbk[:, 0:512], kbuf[:, g, kt * 128:(kt + 1) * 128],
                   QTi[:, 4 * g:4 * g + 4, :].rearrange("p a q -> p (a q)"), True, m is None, [kbuf, QTi])
                if m is not None:
                    mm(sbk, sbk[:, 0:512], identb.ap(), masks[:, m, :], False, True, [identb, masks])
                pt = PT[c1["pt"] % 3]
                c1["pt"] += 1
                S.op("act", lambda e, pt=pt, sbk=sbk: e.activation(out=pt.ap(), in_=sbk[:, 0:512], func=AF.Exp,
                                                                   scale=0.125), reads=[sbk], writes=[pt])
                live[k] = pt
            while post_q:
                emit_post(*post_q.pop(0))
            if k >= LA:
                qi, g, ci, nch, kind, kt, m = steps[k - LA]
                pt = live.pop(k - LA)
                if ci == 0:
                    unit_bank[(qi, g)] = PS[5 + (c1["ol"] % 3)]
                    c1["ol"] += 1
                ob = unit_bank[(qi, g)]
                vbuf = VV if kind == "l" else VVc
                last = ci == nch - 1
                mm(ob, ob[:, 0:256], vbuf[:, kt, g, 0:128], pt[:, 0:256], ci == 0, False, [vbuf, pt], sgc=True)
                mm(ob, ob[:, 0:256], vbuf[:, kt, g, 64:192], pt[:, 256:512], False, last, [vbuf, pt], sgc=True)
                mm(ob, ob[:, 256:512], ones3[:, 0:128], pt[:, 0:256], False, False, [ones3, pt], sgc=True)
                mm(ob, ob[:, 256:512], ones3[:, 64:192], pt[:, 256:512], False, last, [ones3, pt], sgc=True)
                if last:
                    post_q.append((ob, g, qi))
            for po in list(pending_out):
                po[1] -= 1
                if po[1] <= 0:
                    emit_out(po[0])
                    pending_out.remove(po)
        while post_q:
            emit_post(*post_q.pop(0))
        for f_ in side:
            f_()
        rest = [po[0] for po in pending_out]
        return (lambda: [emit_out(q_) for q_ in rest])

    def cache_prep():
        for t in range(2):
            tb = ntb()
            tbv = tb.ap().bitcast(BF16)
            for g in range(4):
                tr(tb, tbv[0:64, g * 128:(g + 1) * 128], ckb[:, t, g * 64:(g + 1) * 64], identb.ap(), [ckb, identb])
            evac(KTc[:, :, t * 128:(t + 1) * 128], tbv[0:64, 0:512].rearrange("p (g q) -> p g q", g=4), [tb], [KTc])
            vv_fill(VVc, t, cvb[:, t, :].rearrange("p (g d) -> p g d", d=64), [cvb])

    QTs, _ = l1_front([XP[0], XP[1]], 0, [0, 1], False, o_nk, o_nv, 0,
                      late_hook=(lambda: (rope_tables(), cache_prep())), cb=0)
    side0 = prep_stages(1, [XL[0], XL[1]], 1, h1T, [0, 128], h1_of, dve_prep=True)
    tail = l1_attn([XP[0], XP[1]], 0, [0, 1], False, QTs, o_yp, 0, side=side0)
    QTs, _ = l1_front([XP[2], XP[3]], 0, [0, 1], False, o_nk, o_nv, 256, mid_hook=tail, cb=256)
    side1 = prep_stages(1, [XL[2], XL[3]], 1, h1T, [256, 384], h1_of, dve_prep=True)
    tail = l1_attn([XP[2], XP[3]], 0, [0, 1], False, QTs, o_yp, 256, side=side1)
    QTs, late = l1_front(XL, 1, [1, 2], True, o_nk, o_nv, 0, mid_hook=tail, cb=0)
    tail = l1_attn(XL, 1, [1, 2], True, QTs, o_ys, 0, side_at=late)
    tail()

    S.emit()
    build_program.stats = dict(n_ops=len(S.ops), n_sems=S.n_sems, sb_hw=S.hw)
    return nc


def _host_consts():
    bf = ml_dtypes.bfloat16
    c = {}
    c["identb"] = np.eye(128, dtype=np.float32).astype(bf)
    c["identf"] = np.eye(2, dtype=np.float32)
    sel = np.zeros((2, 2, 128), np.float32)
    sel[0, 0, :] = 1.0
    sel[1, 1, :] = 1.0
    c["sel"] = sel
    i = np.arange(256, dtype=np.float64)
    ang = 2 * np.pi * np.outer(i, i) / 256.0
    c["dftP"] = np.stack([np.cos(ang) / 16.0, np.sin(ang) / 16.0], axis=1).astype(np.float32).astype(bf)
    c["dftC"] = np.stack([np.cos(ang) / 16.0, -np.sin(ang) / 16.0], axis=1).astype(np.float32).astype(bf)
    return c


def _core_layout(ch):
    own = [2 * ch, 2 * ch + 1]
    left = 2 * ch - 1 if ch > 0 else None
    right = 2 * ch + 2 if ch < 3 else None
    used = set(own) | ({left} if left is not None else set()) | ({right} if right is not None else set())
    spare = [t for t in range(8) if t not in used]
    subs = list(spare)
    l_t = left if left is not None else subs.pop()
    r_t = right if right is not None else subs.pop()
    local = [l_t, own[0], own[1], r_t]
    rest = [t for t in range(8) if t not in local]
    return local, rest, left is not None, right is not None


def _core_consts(ch):
    bf = ml_dtypes.bfloat16
    local, rest, has_l, has_r = _core_layout(ch)
    perm = local + rest
    pos_all = np.concatenate([np.arange(t * 128, (t + 1) * 128) for t in perm]).astype(np.float64)
    pos_loc = pos_all[:512]
    ang = 2 * np.pi * np.outer(pos_all, pos_loc) / 1024.0
    dftS = np.stack([np.cos(ang) / 32.0, np.sin(ang) / 32.0], axis=1).astype(np.float32).astype(bf)
    n_freq = 16
    inv = (10000.0 ** (-np.arange(n_freq, dtype=np.float32) / n_freq)).astype(np.float32)
    row = np.floor(pos_loc / 64).astype(np.float32)
    col = (pos_loc % 64).astype(np.float32)
    a = np.concatenate([row[:, None] * inv, col[:, None] * inv], axis=-1).astype(np.float32)
    cs, sn = np.cos(a), np.sin(a)
    rope = np.stack([np.concatenate([cs, cs], -1), np.concatenate([-sn, sn], -1)], axis=1).astype(np.float32)
    j = np.arange(128)[:, None]
    q = np.arange(128)[None, :]
    ge = (j >= q).astype(np.float32)
    le = (j <= q).astype(np.float32)
    valid = np.stack([ge * (1.0 if has_l else 0.0), ge, le, le * (1.0 if has_r else 0.0)], axis=1)
    masks = np.tile((valid - 1.0) * 30000.0, (1, 1, 4)).astype(np.float32).astype(bf)
    return perm, dftS, rope, masks


_CACHE = {}


def kernel(x_prompt, x_sample, cache_k_l1, cache_v_l1, c, c_ctx,
           norm_w_l0, w_mod_l0, b_mod_l0, w_in_l0, w_out_l0,
           norm_w_l1, w_mod_l1, b_mod_l1, w_in_l1, q_norm_w_l1, k_norm_w_l1, sink_l1, w_out_l1):
    f = lambda a: np.ascontiguousarray(np.asarray(a, dtype=np.float32))
    x_prompt, x_sample = f(x_prompt), f(x_sample)
    cache_k_l1, cache_v_l1, c, c_ctx = f(cache_k_l1), f(cache_v_l1), f(c), f(c_ctx)
    if "nc" not in _CACHE:
        _CACHE["nc"] = build_program()
        _CACHE["hc"] = _host_consts()
        _CACHE["cc"] = [_core_consts(ch) for ch in range(4)]
    nc, hc = _CACHE["nc"], _CACHE["hc"]
    nwT = np.ascontiguousarray(np.stack([f(norm_w_l0).reshape(8, 128).T, f(norm_w_l1).reshape(8, 128).T], axis=1))
    qkw = np.concatenate([f(q_norm_w_l1), f(k_norm_w_l1)])
    shared = dict(wmod0=f(w_mod_l0), bmod0=f(b_mod_l0), win0=f(w_in_l0), wout0=f(w_out_l0),
                  wmod1=f(w_mod_l1), bmod1=f(b_mod_l1), win1=f(w_in_l1), wout1=f(w_out_l1),
                  nwT=nwT, qkw=qkw, sink=f(sink_l1), **hc)
    in_maps = []
    for core in range(NCORES):
        b, ch = core // 4, core % 4
        perm, dftS, rope, masks = _CACHE["cc"][ch]
        xs = np.ascontiguousarray(x_sample[b].reshape(8, 128, 1024)[perm].reshape(1024, 1024))
        m = dict(shared)
        m.update(xp=np.ascontiguousarray(x_prompt[2 * core:2 * core + 2].reshape(512, 1024)), xs=xs,
                 condT=np.ascontiguousarray(np.stack([c_ctx, c[b]], axis=1).reshape(8, 128, 2).transpose(1, 0, 2)),
                 ck=np.ascontiguousarray(cache_k_l1[b].reshape(256, 256)),
                 cv=np.ascontiguousarray(cache_v_l1[b].reshape(256, 256)),
                 dftS=dftS, rope=rope, masks=masks)
        in_maps.append(m)
    res = run_bass_kernel_spmd(nc, in_maps, core_ids=list(range(NCORES)))
    R = res.results
    y_prompt = np.concatenate([R[i]["yp"].reshape(2, 256, 1024) for i in range(NCORES)], axis=0)
    y_sample = np.stack([np.concatenate([R[b * 4 + ch]["ys"] for ch in range(4)], axis=0) for b in range(2)], axis=0)
    new_k = np.concatenate([R[i]["nk"].reshape(2, 256, 4, 64) for i in range(NCORES)], axis=0)
    new_v = np.concatenate([R[i]["nv"].reshape(2, 256, 4, 64) for i in range(NCORES)], axis=0)
    return (y_prompt.astype(np.float32), y_sample.astype(np.float32),
            new_k.astype(np.float32), new_v.astype(np.float32))
```
